# Optimizing a Trainium2 kernel written in Bass

```python
import jax, jax.numpy as jnp
from jax import lax
import numpy as np

D_MODEL = 2048
BATCH = 4
SEQ = 2048
DEPTH = 2
DEC_BATCH = 8
DEC_SEQ = 4
PAST_LEN = 16384
PAGE_SIZE = 128

N_EVEN = (DEPTH + 1) // 2
N_ODD = DEPTH // 2
MIX_WIDTH = D_MODEL
A_WIDTH = MIX_WIDTH // 2
CHUNK = 128
A_GROUPS = 8
A_GROUP_DIM = A_WIDTH // A_GROUPS
SB_HEAD_DIM = 128
SB_HEADS = (MIX_WIDTH // 2) // SB_HEAD_DIM
SB_WIDTH = SB_HEADS * SB_HEAD_DIM
SB_SCALE = SB_HEAD_DIM ** -0.5
SB_BIAS_INIT = -8.0
Q_BLOCK = 128
POOL_WINDOWS = (2, 4, 8, 16)
C_GROUPS = 4
C_WIDTH = MIX_WIDTH // 2
C_GROUP_DIM = C_WIDTH // C_GROUPS
POOL_BUF = max(POOL_WINDOWS) - 1
D_WIDTH = MIX_WIDTH // 2
CONV_WIDTH = 3
FFN_DIM = 5632
EPS = 1e-6
EVEN_IN = 3 * SB_WIDTH + 2 * A_WIDTH
ODD_IN = C_WIDTH + 3 * D_WIDTH

kernel_name = 'hybrid_gmlp_stickbreak_pool_shortconv_step'


def rms_norm(x, g):
    xf = x.astype(jnp.float32)
    y = xf * lax.rsqrt(jnp.mean(xf * xf, axis=-1, keepdims=True) + EPS)
    return (y * g.astype(jnp.float32)).astype(x.dtype)


def swiglu(h, w_gate, w_up, w_down):
    return (jax.nn.silu(h @ w_gate) * (h @ w_up)) @ w_down


def sb_weights(z, mask):
    log_keep = jnp.where(mask, jax.nn.log_sigmoid(-z), 0.0)
    later = lax.cumsum(log_keep, axis=z.ndim - 1, reverse=True) - log_keep
    return jnp.where(mask, jnp.exp(jax.nn.log_sigmoid(z) + later), 0.0)


def sb_prompt(q, k, v, bias):
    b, T, H, Dh = q.shape
    nb = T // Q_BLOCK
    k_pos = jnp.arange(T)
    qb = q.reshape(b, nb, Q_BLOCK, H, Dh).swapaxes(0, 1)
    bias_f = bias.astype(jnp.float32)[None, :, None, None]

    def block(args):
        qi, blk = args
        q_pos = blk * Q_BLOCK + jnp.arange(Q_BLOCK)
        z = jnp.einsum('bqhd,bkhd->bhqk', qi, k).astype(jnp.float32) * SB_SCALE + bias_f
        a = sb_weights(z, k_pos[None, :] < q_pos[:, None])
        return jnp.einsum('bhqk,bkhd->bqhd', a.astype(v.dtype), v)

    out = lax.map(block, (qb, jnp.arange(nb)))
    return out.swapaxes(0, 1).reshape(b, T, H * Dh)


def sb_sample(q, k_new, v_new, k_past, v_past, bias):
    b, Tq, H, Dh = q.shape
    P = k_past.shape[1]
    z = jnp.concatenate([jnp.einsum('bqhd,bkhd->bhqk', q, k_past),
                         jnp.einsum('bqhd,bkhd->bhqk', q, k_new)], axis=-1).astype(jnp.float32) * SB_SCALE
    z = z + bias.astype(jnp.float32)[None, :, None, None]
    q_pos = P + jnp.arange(Tq)
    k_pos = jnp.arange(P + Tq)
    a = sb_weights(z, k_pos[None, :] < q_pos[:, None]).astype(v_new.dtype)
    out = (jnp.einsum('bhqk,bkhd->bqhd', a[..., :P], v_past)
           + jnp.einsum('bhqk,bkhd->bqhd', a[..., P:], v_new))
    return out.reshape(b, Tq, H * Dh)


def chunk_gating(au, av, ws, bs, rows):
    b, T, _ = av.shape
    nc = T // rows
    causal = jnp.tril(jnp.ones((rows, rows), dtype=bool))
    w = jnp.where(causal, ws[:, :rows, :rows], 0.0).astype(av.dtype)
    v = av.reshape(b, nc, rows, A_GROUPS, A_GROUP_DIM)
    mix = jnp.einsum('gpq,bcqgd->bcpgd', w, v) + bs[:, :rows].T[:, :, None]
    return au * mix.reshape(b, T, A_WIDTH)


def even_project(h, w_in, q_g, k_g, av_g):
    b, T, _ = h.shape
    p = h @ w_in
    q, k, v, au, av = jnp.split(p, [SB_WIDTH, 2 * SB_WIDTH, 3 * SB_WIDTH, 3 * SB_WIDTH + A_WIDTH], axis=-1)
    q = rms_norm(q.reshape(b, T, SB_HEADS, SB_HEAD_DIM), q_g)
    k = rms_norm(k.reshape(b, T, SB_HEADS, SB_HEAD_DIM), k_g)
    v = v.reshape(b, T, SB_HEADS, SB_HEAD_DIM)
    au = jax.nn.gelu(au)
    av = rms_norm(jax.nn.gelu(av), av_g)
    return q, k, v, au, av


def pool_mix(xc, prev, pos0, c_wg, c_scale):
    full = xc if prev is None else jnp.concatenate([prev, xc], axis=1)
    b, T, _ = xc.shape
    L = full.shape[1] - T
    ff = full.astype(jnp.float32).reshape(b, L + T, C_GROUPS, C_GROUP_DIM)
    cs = jnp.concatenate([jnp.zeros_like(ff[:, :1]), jnp.cumsum(ff, axis=1)], axis=1)
    win = jnp.array(POOL_WINDOWS, dtype=jnp.int32)
    hi = L + jnp.arange(T) + 1
    lo = jnp.maximum(hi[:, None] - win[None, :], 0)
    s = cs[:, hi] - cs[:, lo, jnp.arange(C_GROUPS)]
    count = jnp.minimum(pos0 + jnp.arange(T)[:, None] + 1, win[None, :]).astype(jnp.float32)
    pooled = (s / count[..., None] - ff[:, L:]).astype(xc.dtype)
    mixed = jnp.einsum('btgi,gio->btgo', pooled, c_wg).reshape(b, T, C_WIDTH) * c_scale
    return mixed, full[:, -POOL_BUF:]


def short_conv(g, prev, w):
    T = g.shape[1]
    full = jnp.concatenate([prev, g], axis=1)
    out = full[:, 0:T] * w[0]
    for tap in range(1, CONV_WIDTH):
        out = out + full[:, tap:tap + T] * w[tap]
    return out, full[:, -(CONV_WIDTH - 1):]


def odd_mixer(h, w_in, w_out, c_wg, c_scale, d_conv_w, pool_prev, conv_prev, pos0):
    p = h @ w_in
    xc, gb, gc, gh = jnp.split(p, [C_WIDTH, C_WIDTH + D_WIDTH, C_WIDTH + 2 * D_WIDTH], axis=-1)
    c_out, pool_state = pool_mix(xc, pool_prev, pos0, c_wg, c_scale)
    conv_out, conv_state = short_conv(gc * gh, conv_prev, d_conv_w)
    d_out = gb * conv_out
    return jnp.concatenate([c_out, d_out], axis=-1) @ w_out, pool_state, conv_state


def setup_inputs(seed: int = 0) -> dict:
    key = jax.random.key(seed)
    ks = jax.random.split(key, 24)
    f32 = jnp.float32
    n_pages = PAST_LEN // PAGE_SIZE
    n_used = DEC_BATCH * n_pages
    n_phys = n_used + max(1, n_used // 4)

    def dense(k, shape, fan_in):
        return jax.random.normal(k, shape, f32) * (fan_in ** -0.5)

    def gain(k, shape):
        return 1.0 + 0.05 * jax.random.normal(k, shape, f32)

    page_table = jax.random.permutation(ks[6], n_phys)[:n_used].reshape(DEC_BATCH, n_pages).astype(jnp.int32)
    return {
        'x_prompt': jax.random.normal(ks[0], (BATCH, SEQ, D_MODEL), f32),
        'x_sample': jax.random.normal(ks[1], (DEC_BATCH, DEC_SEQ, D_MODEL), f32),
        'cache_k': jax.random.normal(ks[2], (N_EVEN, n_phys, PAGE_SIZE, SB_HEADS, SB_HEAD_DIM), f32),
        'cache_v': jax.random.normal(ks[3], (N_EVEN, n_phys, PAGE_SIZE, SB_HEADS, SB_HEAD_DIM), f32),
        'state_pool': jax.random.normal(ks[4], (N_ODD, DEC_BATCH, POOL_BUF, C_WIDTH), f32),
        'state_conv': jax.random.normal(ks[5], (N_ODD, DEC_BATCH, CONV_WIDTH - 1, D_WIDTH), f32),
        'page_table': page_table,
        'norm_g': gain(ks[7], (DEPTH, 3, D_MODEL)),
        'ffn_w_gate': dense(ks[8], (DEPTH, 2, D_MODEL, FFN_DIM), D_MODEL),
        'ffn_w_up': dense(ks[9], (DEPTH, 2, D_MODEL, FFN_DIM), D_MODEL),
        'ffn_w_down': dense(ks[10], (DEPTH, 2, FFN_DIM, D_MODEL), FFN_DIM),
        'w_in_even': dense(ks[11], (N_EVEN, D_MODEL, EVEN_IN), D_MODEL),
        'w_out_even': dense(ks[12], (N_EVEN, A_WIDTH + SB_WIDTH, D_MODEL), A_WIDTH + SB_WIDTH),
        'q_norm_g': gain(ks[13], (N_EVEN, SB_HEAD_DIM)),
        'k_norm_g': gain(ks[14], (N_EVEN, SB_HEAD_DIM)),
        'sb_bias': SB_BIAS_INIT + 0.1 * jax.random.normal(ks[23], (N_EVEN, SB_HEADS), f32),
        'a_v_norm_g': gain(ks[15], (N_EVEN, A_WIDTH)),
        'a_ws': dense(ks[16], (N_EVEN, A_GROUPS, CHUNK, CHUNK), CHUNK),
        'a_bs': 1.0 + 0.02 * jax.random.normal(ks[17], (N_EVEN, A_GROUPS, CHUNK), f32),
        'w_in_odd': dense(ks[18], (N_ODD, D_MODEL, ODD_IN), D_MODEL),
        'w_out_odd': dense(ks[19], (N_ODD, C_WIDTH + D_WIDTH, D_MODEL), C_WIDTH + D_WIDTH),
        'c_wg': dense(ks[20], (N_ODD, C_GROUPS, C_GROUP_DIM, C_GROUP_DIM), C_GROUP_DIM),
        'c_scale': gain(ks[21], (N_ODD, C_WIDTH)),
        'd_conv_w': dense(ks[22], (N_ODD, CONV_WIDTH, D_WIDTH), CONV_WIDTH),
    }


def reference(x_prompt, x_sample, cache_k, cache_v, state_pool, state_conv, page_table,
              norm_g, ffn_w_gate, ffn_w_up, ffn_w_down,
              w_in_even, w_out_even, q_norm_g, k_norm_g, sb_bias, a_v_norm_g, a_ws, a_bs,
              w_in_odd, w_out_odd, c_wg, c_scale, d_conv_w):
    yp, ys = x_prompt, x_sample
    n_pages = page_table.shape[1]
    kp_l, vp_l, ks_l, vs_l, av_l = [], [], [], [], []
    pp_l, ps_l, cp_l, cs_l = [], [], [], []
    for layer in range(DEPTH):
        i = layer // 2
        yp = yp + 0.5 * swiglu(rms_norm(yp, norm_g[layer, 0]), ffn_w_gate[layer, 0], ffn_w_up[layer, 0], ffn_w_down[layer, 0])
        ys = ys + 0.5 * swiglu(rms_norm(ys, norm_g[layer, 0]), ffn_w_gate[layer, 0], ffn_w_up[layer, 0], ffn_w_down[layer, 0])
        hp = rms_norm(yp, norm_g[layer, 1])
        hs = rms_norm(ys, norm_g[layer, 1])
        if layer % 2 == 0:
            q, k, v, au, av = even_project(hp, w_in_even[i], q_norm_g[i], k_norm_g[i], a_v_norm_g[i])
            mix = jnp.concatenate([chunk_gating(au, av, a_ws[i], a_bs[i], CHUNK), sb_prompt(q, k, v, sb_bias[i])], axis=-1)
            yp = yp + mix @ w_out_even[i]
            kp_l.append(k)
            vp_l.append(v)
            q, k, v, au, av = even_project(hs, w_in_even[i], q_norm_g[i], k_norm_g[i], a_v_norm_g[i])
            k_past = cache_k[i][page_table].reshape(DEC_BATCH, n_pages * PAGE_SIZE, SB_HEADS, SB_HEAD_DIM)
            v_past = cache_v[i][page_table].reshape(DEC_BATCH, n_pages * PAGE_SIZE, SB_HEADS, SB_HEAD_DIM)
            mix = jnp.concatenate([chunk_gating(au, av, a_ws[i], a_bs[i], hs.shape[1]),
                                   sb_sample(q, k, v, k_past, v_past, sb_bias[i])], axis=-1)
            ys = ys + mix @ w_out_even[i]
            ks_l.append(k)
            vs_l.append(v)
            av_l.append(av)
        else:
            conv0 = jnp.zeros((hp.shape[0], CONV_WIDTH - 1, D_WIDTH), hp.dtype)
            m, pst, cst = odd_mixer(hp, w_in_odd[i], w_out_odd[i], c_wg[i], c_scale[i], d_conv_w[i], None, conv0, 0)
            yp = yp + m
            pp_l.append(pst)
            cp_l.append(cst)
            m, pst, cst = odd_mixer(hs, w_in_odd[i], w_out_odd[i], c_wg[i], c_scale[i], d_conv_w[i],
                                    state_pool[i], state_conv[i], PAST_LEN)
            ys = ys + m
            ps_l.append(pst)
            cs_l.append(cst)
        yp = yp + 0.5 * swiglu(rms_norm(yp, norm_g[layer, 2]), ffn_w_gate[layer, 1], ffn_w_up[layer, 1], ffn_w_down[layer, 1])
        ys = ys + 0.5 * swiglu(rms_norm(ys, norm_g[layer, 2]), ffn_w_gate[layer, 1], ffn_w_up[layer, 1], ffn_w_down[layer, 1])
    new_k_prompt = jnp.stack(kp_l)
    new_v_prompt = jnp.stack(vp_l)
    new_k_sample = jnp.stack(ks_l)
    new_v_sample = jnp.stack(vs_l)
    new_a_v_sample = jnp.stack(av_l)
    new_pool_prompt = jnp.stack(pp_l)
    new_pool_sample = jnp.stack(ps_l)
    new_conv_prompt = jnp.stack(cp_l)
    new_conv_sample = jnp.stack(cs_l)
    return (yp, ys, new_k_prompt, new_v_prompt, new_k_sample, new_v_sample, new_a_v_sample,
            new_pool_prompt, new_pool_sample, new_conv_prompt, new_conv_sample)
```

```python
import numpy as np
import concourse.bass as bass
import concourse.mybir as mybir
from concourse.bass_utils import run_bass_kernel_spmd

F32 = mybir.dt.float32
BF16 = mybir.dt.bfloat16
I32 = mybir.dt.int32
AF = mybir.ActivationFunctionType
ALU = mybir.AluOpType

N_CORES = 8
_DEV = {"only_l1": False, "skip_ffn": False, "no_attn": False, "max_pages": None, "no_sample": False}
D = 2048
KC = D // 128
FFN = 5632
FC = FFN // 128
TOK = 1024
TT = 512
NTT = TOK // TT
EPS = 1e-6
G = 4
NG = FC // G


class Buf:
    __slots__ = ("name", "w", "r")

    def __init__(self, name):
        self.name = name
        self.w = None
        self.r = {}


class Prog:
    COMPUTE = ("pe", "act", "dve", "pool")
    NRING = 12

    def __init__(self):
        self.ops = {e: [] for e in ("pe", "act", "dve", "pool", "sp")}
        self.cnt = {e: 0 for e in self.COMPUTE}
        self.ring = {"pool": [0] * self.NRING, "sp": [0] * self.NRING, "act": [0] * self.NRING}
        self.ring_next = {"pool": 0, "sp": 0, "act": 0}
        self.out_events = []

    def _deps(self, reads, writes):
        deps = {}

        def add(ev):
            if ev is not None and deps.get(ev[0], 0) < ev[1]:
                deps[ev[0]] = ev[1]
        for b in reads:
            add(b.w)
        for b in writes:
            add(b.w)
            for k, v in b.r.items():
                add((k, v))
        return deps

    def _commit(self, ev, reads, writes):
        for b in reads:
            if b.r.get(ev[0], 0) < ev[1]:
                b.r[ev[0]] = ev[1]
        for b in writes:
            b.w = ev
            b.r = {}

    def op(self, eng, fn, reads=(), writes=()):
        deps = self._deps(reads, writes)
        self.cnt[eng] += 1
        ev = (eng, self.cnt[eng])
        self._commit(ev, reads, writes)
        self.ops[eng].append((fn, deps, ev, 1))
        return ev

    def dma(self, eng, fn, reads=(), writes=(), is_output=False):
        deps = self._deps(reads, writes)
        slot = self.ring_next[eng]
        self.ring_next[eng] = (slot + 1) % self.NRING
        key = ("dma", eng, slot)
        prev = self.ring[eng][slot]
        if prev:
            if deps.get(key, 0) < prev:
                deps[key] = prev
        self.ring[eng][slot] = prev + 16
        ev = (key, prev + 16)
        self._commit(ev, reads, writes)
        self.ops[eng].append((fn, deps, ev, 16))
        if is_output:
            self.out_events.append(ev)
        return ev

    def emit(self, nc, stack):
        sems = {}

        def sem(key):
            if key not in sems:
                nm = key if isinstance(key, str) else "d_%s_%d" % (key[1], key[2])
                sems[key] = stack.enter_context(nc.semaphore("s_" + nm))
            return sems[key]
        for e in self.COMPUTE:
            sem(e)
        for e in self.ring:
            for s in range(self.NRING):
                sem(("dma", e, s))
        final = {}
        for (k, v) in self.out_events:
            final[k] = max(final.get(k, 0), v)
        for e in self.COMPUTE:
            final[e] = self.cnt[e]
        ops = self.ops
        block = stack.enter_context(nc.Block())

        def run(eng_name, eng, extra_final=None):
            waited = {}
            for (fn, deps, ev, inc) in ops[eng_name]:
                for k, v in deps.items():
                    if k == eng_name and eng_name == "pe":
                        continue
                    if waited.get(k, 0) < v:
                        eng.wait_ge(sem(k), v)
                        waited[k] = v
                ins = fn(eng)
                ins.then_inc(sem(ev[0]), inc)
            if extra_final:
                for k, v in extra_final.items():
                    if waited.get(k, 0) < v:
                        eng.wait_ge(sem(k), v)

        @block.tensor
        def _(e):
            run("pe", e)

        @block.scalar
        def _(e):
            run("act", e)

        @block.vector
        def _(e):
            run("dve", e)

        @block.gpsimd
        def _(e):
            run("pool", e)

        @block.sync
        def _(e):
            run("sp", e, extra_final=final)


NB = 128
NT = NB + TOK + 4
TILES = [(0, 512), (512, 512), (1024, NT - 1024)]


class Ctx:
    pass


def alias(new_bufs, old_bufs):
    acc = {}
    for b in old_bufs:
        evs = dict(b.r)
        if b.w is not None:
            evs[b.w[0]] = max(evs.get(b.w[0], 0), b.w[1])
        for k, v in evs.items():
            if acc.get(k, 0) < v:
                acc[k] = v
    for b in new_bufs:
        for k, v in acc.items():
            if b.r.get(k, 0) < v:
                b.r[k] = v


def emit_norm(P, C, gidx, tiles):
    alias([b for l in C.h_b for b in l], [b for l in C.m_b for b in l])
    for ti, (c0, cn) in enumerate(tiles):
        bank = C.next_bank()
        for kc in range(KC):
            sq, sqb = C.sq[kc % 2], C.sq_b[kc % 2]
            P.op("act", lambda e, kc=kc, sq=sq, c0=c0, cn=cn: e.activation(
                out=sq[:, 0:cn], in_=C.xT[:, kc, c0:c0 + cn], func=AF.Square),
                reads=[C.x_b[kc][ti]], writes=[sqb])
            P.op("pe", lambda e, kc=kc, sq=sq, cn=cn, bank=bank: e.matmul(
                C.ps[bank][:, 0:cn], lhsT=C.ones32[:, :], rhs=sq[:, 0:cn],
                start=(kc == 0), stop=(kc == KC - 1)),
                reads=[sqb], writes=[C.ps_b[bank]])
        P.op("act", lambda e, cn=cn, bank=bank: e.activation(
            out=C.rstd[:, 0:cn], in_=C.ps[bank][:, 0:cn], func=AF.Sqrt, bias=C.epsc[:, 0:1], scale=1.0),
            reads=[C.ps_b[bank], C.const_b], writes=[C.rstd_b])
        P.op("dve", lambda e, cn=cn: e.reciprocal(out=C.rstd[:, 0:cn], in_=C.rstd[:, 0:cn]),
             reads=[C.rstd_b], writes=[C.rstd_b])
        for kc in range(KC):
            P.op("dve", lambda e, kc=kc, c0=c0, cn=cn: e.scalar_tensor_tensor(
                out=C.hT[:, kc, c0:c0 + cn], in0=C.xT[:, kc, c0:c0 + cn],
                scalar=C.gains[:, gidx, kc:kc + 1], in1=C.rstd[:, 0:cn],
                op0=ALU.mult, op1=ALU.mult),
                reads=[C.x_b[kc][ti], C.rstd_b], writes=[C.h_b[kc][ti]])


def emit_ffn(P, C, fidx, gidx, tiles):
    emit_norm(P, C, gidx, tiles)
    alias([b for l in C.a_b for b in l] + C.wgu_b, C.arena_mixer_bufs + C.l1_arena_bufs)
    alias([C.wd_b[1]], C.oR_b + [C.t16_b, C.ost_b])
    wgu_d, wd_d = C.wgu_dram, C.wd_dram

    def ug(g):
        gb = g % 2
        for fl in range(G):
            fc = g * G + fl
            slot = C.wgu_next
            C.wgu_next = (slot + 1) % len(C.wgu)
            P.dma("pool", lambda e, slot=slot, fc=fc: e.dma_start(
                out=C.wgu[slot][:, :], in_=wgu_d[fidx, fc, :, :]), writes=[C.wgu_b[slot]])
            for ti, (c0, cn) in enumerate(tiles):
                bg, bu = C.next_bank(), C.next_bank()
                for which, bank in ((0, bg), (1, bu)):
                    def mm(e, slot=slot, which=which, bank=bank, c0=c0, cn=cn):
                        ins = None
                        for kc in range(KC):
                            off = (kc * 2 + which) * 128
                            ins = e.matmul(C.ps[bank][:, 0:cn], lhsT=C.wgu[slot][:, off:off + 128],
                                           rhs=C.hT[:, kc, c0:c0 + cn],
                                           start=(kc == 0), stop=(kc == KC - 1))
                        return ins
                    P.op("pe", mm, reads=[C.wgu_b[slot]] + [C.h_b[kc][ti] for kc in range(KC)],
                         writes=[C.ps_b[bank]])
                sg = C.sg_next
                C.sg_next = (sg + 1) % len(C.sg)
                P.op("act", lambda e, sg=sg, bg=bg, cn=cn: e.activation(
                    out=C.sg[sg][:, 0:cn], in_=C.ps[bg][:, 0:cn], func=AF.Silu),
                    reads=[C.ps_b[bg]], writes=[C.sg_b[sg]])
                P.op("dve", lambda e, sg=sg, bu=bu, gb=gb, fl=fl, c0=c0, cn=cn: e.tensor_tensor(
                    out=C.aT[gb][:, fl, c0:c0 + cn], in0=C.ps[bu][:, 0:cn], in1=C.sg[sg][:, 0:cn],
                    op=ALU.mult), reads=[C.ps_b[bu], C.sg_b[sg]], writes=[C.a_b[gb][fl]])

    def down(g):
        gb = g % 2
        P.dma("pool", lambda e, gb=gb, g=g: e.dma_start(
            out=C.wd[gb][:, :, :], in_=wd_d[fidx, g, :, :, :]), writes=[C.wd_b[gb]])
        for dc in range(KC):
            for ti, (c0, cn) in enumerate(tiles):
                bank = C.next_bank()

                def mm(e, gb=gb, dc=dc, bank=bank, c0=c0, cn=cn):
                    ins = None
                    for fl in range(G):
                        ins = e.matmul(C.ps[bank][:, 0:cn], lhsT=C.wd[gb][:, fl, dc * 128:(dc + 1) * 128],
                                       rhs=C.aT[gb][:, fl, c0:c0 + cn],
                                       start=(fl == 0), stop=(fl == G - 1))
                    return ins
                P.op("pe", mm, reads=[C.wd_b[gb]] + C.a_b[gb], writes=[C.ps_b[bank]])
                P.op("dve", lambda e, dc=dc, bank=bank, c0=c0, cn=cn: e.scalar_tensor_tensor(
                    out=C.xT[:, dc, c0:c0 + cn], in0=C.ps[bank][:, 0:cn], scalar=0.5,
                    in1=C.xT[:, dc, c0:c0 + cn], op0=ALU.mult, op1=ALU.add),
                    reads=[C.ps_b[bank], C.x_b[dc][ti]], writes=[C.x_b[dc][ti]])

    for g in range(NG):
        ug(g)
        if g >= 1:
            down(g - 1)
    down(NG - 1)


BLOCKS = [(i * 128, 128) for i in range(9)] + [(NB + TOK, 4)]
SB_SCALE = 128 ** -0.5
ACW = 5 * 128 + 8 + 1 + 32


GELU_C = 1.5957691216057308


def emit_gelu_tanh(P, C, out_ap, out_b, ps_ap, ps_b, sqi, cn, part_all=False):
    t = C.sq[sqi][:, 0:cn] if part_all else C.sq[sqi][0:cn, :]
    tb = C.sq_b[sqi]
    P.op("act", lambda e: e.activation(out=t, in_=ps_ap, func=AF.Square), reads=[ps_b], writes=[tb])
    P.op("dve", lambda e: e.tensor_scalar(out=t, in0=t, scalar1=0.044715, scalar2=1.0,
                                          op0=ALU.mult, op1=ALU.add), reads=[tb], writes=[tb])
    P.op("dve", lambda e: e.tensor_tensor(out=t, in0=t, in1=ps_ap, op=ALU.mult), reads=[tb, ps_b], writes=[tb])
    P.op("act", lambda e: e.activation(out=t, in_=t, func=AF.Sigmoid, scale=GELU_C), reads=[tb], writes=[tb])
    P.op("dve", lambda e: e.tensor_tensor(out=out_ap, in0=t, in1=ps_ap, op=ALU.mult),
         reads=[tb, ps_b], writes=[out_b])


OTHER_ROWS = 896


def emit_kv(P, C, blocks, row0, to_outputs, proj_group, load_sec):
    for sec in (2, 3, 4, 5):
        slot = sec % 2
        half = sec % 2
        load_sec(sec, slot)
        for bi, (c0, cn) in enumerate(blocks):
            bank = C.next_bank()
            proj_group(slot, bank, c0, cn)
            st = C.stage_next
            C.stage_next = (st + 1) % len(C.stage)
            if sec in (2, 3):
                sqi = bi % 2
                P.op("act", lambda e, sqi=sqi, bank=bank, cn=cn: e.activation(
                    out=C.sq[sqi][0:cn, :], in_=C.ps[bank][0:cn, :], func=AF.Square),
                    reads=[C.ps_b[bank]], writes=[C.sq_b[sqi]])
                P.op("dve", lambda e, sqi=sqi, cn=cn: e.tensor_reduce(
                    out=C.ssq[0:cn, 0:4], in_=C.sq[sqi][0:cn, :].rearrange("p (h d) -> p h d", d=128),
                    axis=mybir.AxisListType.X, op=ALU.add), reads=[C.sq_b[sqi]], writes=[C.ssq_b])
                P.op("act", lambda e, cn=cn: e.activation(
                    out=C.ssq[0:cn, 0:4], in_=C.ssq[0:cn, 0:4], func=AF.Sqrt, bias=C.epsc[0:cn, 0:1],
                    scale=1.0 / 128), reads=[C.ssq_b, C.const_b], writes=[C.ssq_b])
                P.op("dve", lambda e, cn=cn: e.reciprocal(out=C.ssq[0:cn, 0:4], in_=C.ssq[0:cn, 0:4]),
                     reads=[C.ssq_b], writes=[C.ssq_b])
                for h in range(4):
                    P.op("dve", lambda e, h=h, st=st, bank=bank, cn=cn: e.scalar_tensor_tensor(
                        out=C.stage[st][0:cn, h * 128:(h + 1) * 128], in0=C.ps[bank][0:cn, h * 128:(h + 1) * 128],
                        scalar=C.ssq[0:cn, h:h + 1], in1=C.gk[0:cn, :], op0=ALU.mult, op1=ALU.mult),
                        reads=[C.ps_b[bank], C.ssq_b, C.const_b], writes=[C.stage_b[st]])
                dst = C.k_out
            else:
                P.op("act", lambda e, st=st, bank=bank, cn=cn: e.activation(
                    out=C.stage[st][0:cn, :], in_=C.ps[bank][0:cn, :], func=AF.Copy),
                    reads=[C.ps_b[bank]], writes=[C.stage_b[st]])
                dst = C.v_out
            if to_outputs:
                P.dma("sp", lambda e, st=st, dst=dst, c0=c0, cn=cn, half=half: e.dma_start(
                    out=dst[c0:c0 + cn, half * 512:(half + 1) * 512], in_=C.stage[st][0:cn, :]),
                    reads=[C.stage_b[st]], is_output=True)
            scr, scrb = (C.kscr, C.kscr_b) if sec in (2, 3) else (C.vscr, C.vscr_b)
            P.dma("sp", lambda e, st=st, scr=scr, c0=c0, cn=cn, half=half: e.dma_start(
                out=scr[row0 + c0:row0 + c0 + cn, half * 512:(half + 1) * 512], in_=C.stage[st][0:cn, :]),
                reads=[C.stage_b[st]], writes=[scrb])


def emit_even_proj(P, C, tiles=None, only_kv_blocks=None):
    emit_norm(P, C, 1, tiles or TILES)
    ffn_bufs = [b for l in C.a_b for b in l] + C.wgu_b
    alias(C.mix_b_all, ffn_bufs)

    def proj_group(slot, bank, c0, cn):
        ti = min(c0 // 512, 2)

        def mm(e):
            ins = None
            for kc in range(KC):
                ins = e.matmul(C.ps[bank][0:cn, :], lhsT=C.hT[:, kc, c0:c0 + cn],
                               rhs=C.wd[slot][:, kc // 4, (kc % 4) * 512:(kc % 4) * 512 + 512],
                               start=(kc == 0), stop=(kc == KC - 1))
            return ins
        P.op("pe", mm, reads=[C.wd_b[slot]] + [C.h_b[kc][ti] for kc in range(KC)], writes=[C.ps_b[bank]])

    def load_sec(sec, slot):
        P.dma("pool", lambda e: e.dma_start(out=C.wd[slot][:, :, :], in_=C.win_tm[sec, :, :, :]),
              writes=[C.wd_b[slot]])

    C.proj_group, C.load_sec = proj_group, load_sec
    if not only_kv_blocks:
        emit_kv(P, C, BLOCKS, OTHER_ROWS, True, proj_group, load_sec)
    else:
        emit_kv(P, C, only_kv_blocks, 0, False, proj_group, load_sec)
        return

    load_sec(6, 0)
    load_sec(7, 1)
    for bi, (c0, cn) in enumerate(BLOCKS):
        banks = (C.next_bank(), C.next_bank())
        for half in range(2):
            proj_group(half, banks[half], c0, cn)
            emit_gelu_tanh(P, C, C.gel[0:cn, half * 512:(half + 1) * 512], C.gel_b,
                           C.ps[banks[half]][0:cn, :], C.ps_b[banks[half]], half, cn)
        for half in range(2):
            P.op("act", lambda e, half=half, cn=cn: e.activation(
                out=C.sq[half][0:cn, :], in_=C.gel[0:cn, half * 512:(half + 1) * 512], func=AF.Square),
                reads=[C.gel_b], writes=[C.sq_b[half]])
            P.op("dve", lambda e, half=half, cn=cn: e.tensor_reduce(
                out=C.ssq[0:cn, 4 + half:5 + half], in_=C.sq[half][0:cn, :],
                axis=mybir.AxisListType.X, op=ALU.add), reads=[C.sq_b[half]], writes=[C.ssq_b])
        P.op("dve", lambda e, cn=cn: e.tensor_tensor(
            out=C.ssq[0:cn, 6:7], in0=C.ssq[0:cn, 4:5], in1=C.ssq[0:cn, 5:6], op=ALU.add),
            reads=[C.ssq_b], writes=[C.ssq_b])
        P.op("act", lambda e, cn=cn: e.activation(
            out=C.ssq[0:cn, 6:7], in_=C.ssq[0:cn, 6:7], func=AF.Sqrt, bias=C.epsc[0:cn, 0:1],
            scale=1.0 / 1024), reads=[C.ssq_b, C.const_b], writes=[C.ssq_b])
        P.op("dve", lambda e, cn=cn: e.reciprocal(out=C.ssq[0:cn, 6:7], in_=C.ssq[0:cn, 6:7]),
             reads=[C.ssq_b], writes=[C.ssq_b])
        P.op("dve", lambda e, bi=bi, cn=cn: e.scalar_tensor_tensor(
            out=C.avn[0:cn, bi, :], in0=C.gel[0:cn, :], scalar=C.ssq[0:cn, 6:7], in1=C.gav[0:cn, :],
            op0=ALU.mult, op1=ALU.mult), reads=[C.gel_b, C.ssq_b, C.const_b], writes=[C.avn_b[bi]])
        if cn == 4:
            P.op("dve", lambda e, cn=cn: e.scalar_tensor_tensor(
                out=C.gel[0:cn, :], in0=C.gel[0:cn, :], scalar=C.ssq[0:cn, 6:7], in1=C.gav[0:cn, :],
                op0=ALU.mult, op1=ALU.mult), reads=[C.gel_b, C.ssq_b, C.const_b], writes=[C.gel_b])
            P.dma("sp", lambda e, cn=cn: e.dma_start(out=C.avn_out[0:cn, :], in_=C.gel[0:cn, :]),
                  reads=[C.gel_b], is_output=True)


def emit_even_gate(P, C):
    def proj_group(slot, bank, c0, cn):
        ti = min(c0 // 512, 2)

        def mm(e):
            ins = None
            for kc in range(KC):
                ins = e.matmul(C.ps[bank][0:cn, :], lhsT=C.hT[:, kc, c0:c0 + cn],
                               rhs=C.wd[slot][:, kc // 4, (kc % 4) * 512:(kc % 4) * 512 + 512],
                               start=(kc == 0), stop=(kc == KC - 1))
            return ins
        P.op("pe", mm, reads=[C.wd_b[slot]] + [C.h_b[kc][ti] for kc in range(KC)], writes=[C.ps_b[bank]])

    for sec in (0, 1):
        slot = sec % 2
        P.dma("pool", lambda e, sec=sec, slot=slot: e.dma_start(out=C.wd[slot][:, :, :], in_=C.win_tm[sec, :, :, :]),
              writes=[C.wd_b[slot]])
        for bi, (c0, cn) in enumerate(BLOCKS):
            bank = C.next_bank()
            proj_group(slot, bank, c0, cn)
            st = C.stage_next
            C.stage_next = (st + 1) % len(C.stage)
            sqi = bi % 2
            P.op("act", lambda e, sqi=sqi, bank=bank, cn=cn: e.activation(
                out=C.sq[sqi][0:cn, :], in_=C.ps[bank][0:cn, :], func=AF.Square),
                reads=[C.ps_b[bank]], writes=[C.sq_b[sqi]])
            P.op("dve", lambda e, sqi=sqi, cn=cn: e.tensor_reduce(
                out=C.ssq[0:cn, 0:4], in_=C.sq[sqi][0:cn, :].rearrange("p (h d) -> p h d", d=128),
                axis=mybir.AxisListType.X, op=ALU.add), reads=[C.sq_b[sqi]], writes=[C.ssq_b])
            P.op("act", lambda e, cn=cn: e.activation(
                out=C.ssq[0:cn, 0:4], in_=C.ssq[0:cn, 0:4], func=AF.Sqrt, bias=C.epsc[0:cn, 0:1],
                scale=1.0 / 128), reads=[C.ssq_b, C.const_b], writes=[C.ssq_b])
            P.op("dve", lambda e, cn=cn: e.reciprocal(out=C.ssq[0:cn, 0:4], in_=C.ssq[0:cn, 0:4]),
                 reads=[C.ssq_b], writes=[C.ssq_b])
            P.op("dve", lambda e, cn=cn: e.tensor_single_scalar(
                out=C.ssq[0:cn, 0:4], in_=C.ssq[0:cn, 0:4], scalar=SB_SCALE, op=ALU.mult),
                reads=[C.ssq_b], writes=[C.ssq_b])
            for h in range(4):
                P.op("dve", lambda e, h=h, st=st, bank=bank, cn=cn: e.scalar_tensor_tensor(
                    out=C.stage[st][0:cn, h * 128:(h + 1) * 128], in0=C.ps[bank][0:cn, h * 128:(h + 1) * 128],
                    scalar=C.ssq[0:cn, h:h + 1], in1=C.gq[0:cn, :], op0=ALU.mult, op1=ALU.mult),
                    reads=[C.ps_b[bank], C.ssq_b, C.const_b], writes=[C.stage_b[st]])
            P.dma("sp", lambda e, st=st, c0=c0, cn=cn, sec=sec: e.dma_start(
                out=C.qscr[c0:c0 + cn, sec * 512:(sec + 1) * 512], in_=C.stage[st][0:cn, :]),
                reads=[C.stage_b[st]], writes=[C.qscr_b])

    for hf in range(2):
        P.dma("sp", lambda e, hf=hf: e.dma_start(out=C.sq[hf][:, :], in_=C.wsT_d[:, hf * 512:(hf + 1) * 512]),
              writes=[C.sq_b[hf]])
    for g in range(8):
        P.op("dve", lambda e, g=g: e.tensor_tensor(out=C.wmT[:, g, :], in0=C.sq[g // 4][:, (g % 4) * 128:(g % 4 + 1) * 128],
                                                   in1=C.gconst[:, 1024:1152], op=ALU.mult),
             reads=[C.const_b, C.sq_b[g // 4]], writes=[C.wmT_b])

    alias(C.au_b, C.stage_b + [C.gel_b])
    for gq in range(2):
        slot = gq % 2
        P.dma("pool", lambda e, gq=gq, slot=slot: e.dma_start(out=C.wd[slot][:, :, :], in_=C.wau[gq, :, :, :]),
              writes=[C.wd_b[slot]])
        for ocl in range(4):
            for ti, (c0, cn) in enumerate(TILES):
                bank = C.next_bank()

                def mm(e, slot=slot, ocl=ocl, bank=bank, c0=c0, cn=cn):
                    ins = None
                    for kc in range(KC):
                        off = (kc * 4 + ocl) * 128
                        ins = e.matmul(C.ps[bank][:, 0:cn], lhsT=C.wd[slot][:, off // 2048, off % 2048:off % 2048 + 128],
                                       rhs=C.hT[:, kc, c0:c0 + cn], start=(kc == 0), stop=(kc == KC - 1))
                    return ins
                P.op("pe", mm, reads=[C.wd_b[slot]] + [C.h_b[kc][ti] for kc in range(KC)], writes=[C.ps_b[bank]])
                emit_gelu_tanh(P, C, C.auT[:, gq * 4 + ocl, c0:c0 + cn], C.au_b[gq * 4 + ocl], C.ps[bank][:, 0:cn],
                               C.ps_b[bank], (ocl + ti) % 2, cn, part_all=True)

    alias([b for l in C.m_b for b in l], [b for l in C.h_b for b in l])
    for gq in range(2):
        for bi, (c0, cn) in enumerate(BLOCKS):
            bank = C.next_bank()

            def mm(e, bi=bi, bank=bank, cn=cn, gq=gq):
                ins = None
                for gl in range(4):
                    g = gq * 4 + gl
                    ins = e.matmul(C.ps[bank][:, gl * 128:gl * 128 + cn], lhsT=C.avn[0:cn, bi, g * 128:(g + 1) * 128],
                                   rhs=C.wmT[0:cn, g, 0:cn], start=True, stop=True)
                return ins
            P.op("pe", mm, reads=[C.avn_b[bi], C.wmT_b], writes=[C.ps_b[bank]])
            sqi = bi % 2
            pv = C.ps[bank][:, :].rearrange("p (g c) -> p g c", g=4)[:, :, 0:cn]
            tv = C.sq[sqi][:, :].rearrange("p (g c) -> p g c", g=4)[:, :, 0:cn]
            bv = C.bs_bc[:, gq * 512:(gq + 1) * 512].rearrange("p (g c) -> p g c", g=4)[:, :, 0:cn]
            P.op("dve", lambda e, pv=pv, tv=tv, bv=bv: e.tensor_tensor(out=tv, in0=pv, in1=bv, op=ALU.add),
                 reads=[C.ps_b[bank], C.const_b], writes=[C.sq_b[sqi]])
            ti = min(c0 // 512, 2)
            P.op("dve", lambda e, tv=tv, gq=gq, c0=c0, cn=cn: e.tensor_tensor(
                out=C.mixT[:, gq * 4:gq * 4 + 4, c0:c0 + cn], in0=tv, in1=C.auT[:, gq * 4:gq * 4 + 4, c0:c0 + cn],
                op=ALU.mult), reads=[C.sq_b[sqi]] + C.au_b[gq * 4:gq * 4 + 4],
                writes=[C.m_b[gq * 4 + gl][ti] for gl in range(4)])


NKEYB = 16
OTHER = 896


def emit_attn_prompt(P, C):
    QT = [(0, 7), (384, 10), (768, 13)]
    head_b = [C.kh_b, C.vh_b, C.qh_b, C.khT_b, C.qhT_b, C.e_b[0], C.e_b[1], C.sp_b[0], C.sp_b[1],
              C.a_b2[0], C.a_b2[1], C.sp32_b, C.spbf_b]
    alias(head_b, C.avn_b + C.au_b + [C.wmT_b] + C.stage_b + [C.gel_b])
    for h in range(8):
        hs = slice(h * 128, (h + 1) * 128)
        P.dma("pool", lambda e, hs=hs: e.dma_start(
            out=C.kh[:, :, :], in_=C.kscr[0:2048, hs].rearrange("(b p) d -> p b d", p=128)),
            reads=[C.kscr_b], writes=[C.kh_b])
        P.dma("pool", lambda e, hs=hs: e.dma_start(
            out=C.vh[:, :, :], in_=C.vscr[0:2048, hs].rearrange("(b p) d -> p b d", p=128)),
            reads=[C.vscr_b], writes=[C.vh_b])
        P.dma("pool", lambda e, hs=hs: e.dma_start(
            out=C.qh[:, :, :], in_=C.qscr[0:1152, hs].rearrange("(b p) d -> p b d", p=128)),
            reads=[C.qscr_b], writes=[C.qh_b])
        for (src, srcb, dst, dstb, nblk) in ((C.kh, C.kh_b, C.khT, C.khT_b, 16), (C.qh, C.qh_b, C.qhT, C.qhT_b, 9)):
            for b0 in range(0, nblk, 4):
                nb = min(4, nblk - b0)
                bank = C.next_bank()
                pbf = C.ps[bank][:, :].bitcast(BF16)

                def tr(e, src=src, b0=b0, nb=nb, pbf=pbf):
                    ins = None
                    for i in range(nb):
                        ins = e.transpose(pbf[:, i * 128:(i + 1) * 128], src[:, b0 + i, :], C.ident[:, :])
                    return ins
                P.op("pe", tr, reads=[srcb, C.const_b], writes=[C.ps_b[bank]])
                P.op("dve", lambda e, dst=dst, b0=b0, nb=nb, pbf=pbf: e.tensor_copy(
                    out=dst[:, b0 * 128:(b0 + nb) * 128], in_=pbf[:, 0:nb * 128]),
                    reads=[C.ps_b[bank]], writes=[dstb])
        bias = C.sbias[:, h:h + 1]
        for (q0, Qa) in QT:
            P.op("pool", lambda e: e.memset(C.sp32[:, :], 0.0), writes=[C.sp32_b])
            obank = C.next_bank()
            P.op("pe", lambda e, obank=obank: e.matmul(C.ps[obank][:, 0:384], lhsT=C.zerosb[:, :], rhs=C.khT[:, 0:384],
                                                       start=True, stop=False),
                 reads=[C.const_b, C.khT_b], writes=[C.ps_b[obank]])
            Stop = Qa + 2
            for ui, S in enumerate(range(Stop, -1, -1)):
                j0 = max(S - Qa, 0) * 128
                w = 384 - j0
                diag = S >= Qa
                kblk = C.khT[:, S * 128:(S + 1) * 128]
                qcols = C.qhT[:, q0 + j0:q0 + 384]
                i2 = ui % 2
                zb = C.next_bank()
                if zb == obank:
                    zb = C.next_bank()
                P.op("pe", lambda e, zb=zb, w=w, kblk=kblk, qcols=qcols: e.matmul(
                    C.ps[zb][:, 0:w], lhsT=kblk, rhs=qcols, start=True, stop=True),
                    reads=[C.khT_b, C.qhT_b], writes=[C.ps_b[zb]])
                P.op("act", lambda e, zb=zb, w=w, i2=i2, bias=bias: e.activation(
                    out=C.ebuf[i2][:, 0:w], in_=C.ps[zb][:, 0:w], func=AF.Exp, bias=bias, scale=1.0),
                    reads=[C.ps_b[zb], C.const_b], writes=[C.e_b[i2]])
                P.op("act", lambda e, w=w, i2=i2: e.activation(
                    out=C.spb[i2][:, 0:w], in_=C.ebuf[i2][:, 0:w], func=AF.Ln, bias=C.onec[:, 0:1], scale=1.0),
                    reads=[C.e_b[i2], C.const_b], writes=[C.sp_b[i2]])
                if diag:
                    P.op("dve", lambda e, i2=i2: e.tensor_tensor(
                        out=C.spb[i2][:, 0:128], in0=C.spb[i2][:, 0:128], in1=C.tris[:, :], op=ALU.mult),
                        reads=[C.sp_b[i2], C.const_b], writes=[C.sp_b[i2]])
                ab = C.next_bank()
                if ab == obank:
                    ab = C.next_bank()

                def mm2(e, ab=ab, w=w, kblk=kblk, qcols=qcols, i2=i2, j0=j0, first=(S == Stop)):
                    e.matmul(C.ps[ab][:, 0:w], lhsT=kblk, rhs=qcols, start=True, stop=False)
                    ins = e.matmul(C.ps[ab][:, 0:w], lhsT=C.ntri[:, :], rhs=C.spb[i2][:, 0:w], start=False, stop=first)
                    if not first:
                        ins = e.matmul(C.ps[ab][:, 0:w], lhsT=C.nones[:, :], rhs=C.spbf[:, j0:384], start=False, stop=True)
                    return ins
                P.op("pe", mm2, reads=[C.khT_b, C.qhT_b, C.sp_b[i2], C.spbf_b, C.const_b], writes=[C.ps_b[ab]])
                P.op("act", lambda e, ab=ab, w=w, i2=i2, bias=bias: e.activation(
                    out=C.abuf[i2][:, 0:w], in_=C.ps[ab][:, 0:w], func=AF.Exp, bias=bias, scale=1.0),
                    reads=[C.ps_b[ab], C.const_b], writes=[C.a_b2[i2]])
                if diag:
                    P.op("dve", lambda e, i2=i2: e.tensor_tensor(
                        out=C.abuf[i2][:, 0:128], in0=C.abuf[i2][:, 0:128], in1=C.tris[:, :], op=ALU.mult),
                        reads=[C.a_b2[i2], C.const_b], writes=[C.a_b2[i2]])
                P.op("pe", lambda e, obank=obank, S=S, j0=j0, w=w, i2=i2, last=(S == 0): e.matmul(
                    C.ps[obank][:, j0:384], lhsT=C.vh[:, S, :], rhs=C.abuf[i2][:, 0:w], start=False, stop=last),
                    reads=[C.vh_b, C.a_b2[i2]], writes=[C.ps_b[obank]])
                if S > 0:
                    P.op("pool", lambda e, j0=j0, w=w, i2=i2: e.tensor_tensor(
                        out=C.sp32[:, j0:384], in0=C.sp32[:, j0:384], in1=C.spb[i2][:, 0:w], op=ALU.add),
                        reads=[C.sp_b[i2], C.sp32_b], writes=[C.sp32_b])
                    P.op("pool", lambda e: e.tensor_copy(out=C.spbf[:, :], in_=C.sp32[:, :]),
                         reads=[C.sp32_b], writes=[C.spbf_b])
            for t3 in range(3):
                c0 = q0 + t3 * 128
                ti = min(c0 // 512, 2)
                P.op("act", lambda e, obank=obank, t3=t3, c0=c0, h=h: e.activation(
                    out=C.mixT[:, 8 + h, c0:c0 + 128], in_=C.ps[obank][:, t3 * 128:(t3 + 1) * 128], func=AF.Copy),
                    reads=[C.ps_b[obank]], writes=[C.m_b[8 + h][ti]])


def emit_attn_sample(P, C):
    NP = C.n_pages if _DEV["max_pages"] is None else _DEV["max_pages"]
    sb_ = [C.kpg32_b[0], C.kpg32_b[1], C.vpg32_b[0], C.vpg32_b[1], C.kpg_b[0], C.kpg_b[1], C.vpg_b[0], C.vpg_b[1],
           C.ktp_b[0], C.ktp_b[1], C.s_small_b, C.sz_b[0], C.sz_b[1], C.ssp_b[0], C.ssp_b[1], C.sab_b[0], C.sab_b[1],
           C.ssp32_b, C.sspbf_b, C.idx_b]
    alias(sb_, C.arena_mixer_bufs_wo_sample)
    P.dma("sp", lambda e: e.dma_start(out=C.pt_i[:, :], in_=C.pt_d[:, :]), writes=[C.idx_b])
    P.op("dve", lambda e: e.tensor_copy(out=C.pt_f[:, :], in_=C.pt_i[:, :]), reads=[C.idx_b], writes=[C.idx_b])
    P.op("dve", lambda e: e.tensor_scalar(out=C.pt_f[:, :], in0=C.pt_f[:, :], scalar1=128.0, scalar2=C.aconst32[:, 648:649],
                                          op0=ALU.mult, op1=ALU.add), reads=[C.idx_b, C.const_b], writes=[C.idx_b])
    P.op("dve", lambda e: e.tensor_copy(out=C.pt_i[:, :], in_=C.pt_f[:, :]), reads=[C.idx_b], writes=[C.idx_b])
    P.dma("pool", lambda e: e.dma_start(out=C.sq_new[0:4, :], in_=C.qscr[NB + TOK:NT, :]), reads=[C.qscr_b], writes=[C.s_small_b])
    P.dma("pool", lambda e: e.dma_start(out=C.sk_new[0:4, :], in_=C.kscr[OTHER_ROWS + NB + TOK:OTHER_ROWS + NT, :]),
          reads=[C.kscr_b], writes=[C.s_small_b])
    P.dma("pool", lambda e: e.dma_start(out=C.sv_new[0:4, :], in_=C.vscr[OTHER_ROWS + NB + TOK:OTHER_ROWS + NT, :]),
          reads=[C.vscr_b], writes=[C.s_small_b])
    for (src, dst) in ((C.sq_new, C.sqT), (C.sk_new, C.skT)):
        bank = C.next_bank()
        pbf = C.ps[bank][:, :].bitcast(BF16)

        def tr(e, src=src, pbf=pbf):
            ins = None
            for h in range(8):
                ins = e.transpose(pbf[:, h * 4:(h + 1) * 4], src[0:4, h * 128:(h + 1) * 128], C.ident[0:4, 0:4])
            return ins
        P.op("pe", tr, reads=[C.s_small_b, C.const_b], writes=[C.ps_b[bank]])
        P.op("dve", lambda e, dst=dst, pbf=pbf: e.tensor_copy(out=dst[:, :], in_=pbf[:, 0:32]),
             reads=[C.ps_b[bank]], writes=[C.s_small_b])
    for h in range(8):
        P.op("dve", lambda e, h=h: e.tensor_copy(out=C.mask4[0:4, h * 4:(h + 1) * 4], in_=C.tris[0:4, 0:4]),
             reads=[C.const_b], writes=[C.s_small_b])
    P.op("pool", lambda e: e.memset(C.ssp32[:, :], 0.0), writes=[C.ssp32_b])
    P.op("pool", lambda e: e.memset(C.sspbf[:, :], 0.0), writes=[C.sspbf_b])
    obank = C.next_bank()
    P.op("pe", lambda e: e.matmul(C.ps[obank][:, 0:32], lhsT=C.zerosb[:, :], rhs=C.sqT[:, :], start=True, stop=False),
         reads=[C.const_b, C.s_small_b], writes=[C.ps_b[obank]])

    def nb(avoid):
        b = C.next_bank()
        while b in avoid:
            b = C.next_bank()
        return b

    def unit(ui, np_, kT_of, kT_b, v_of, v_b, diag, first, last):
        i2 = ui % 2
        zb = nb((obank,))

        def mmz(e):
            ins = None
            for h in range(8):
                ins = e.matmul(C.ps[zb][0:np_, h * 4:(h + 1) * 4], lhsT=kT_of(h), rhs=C.sqT[:, h * 4:(h + 1) * 4],
                               start=True, stop=True)
            return ins
        P.op("pe", mmz, reads=[kT_b, C.s_small_b], writes=[C.ps_b[zb]])
        P.op("dve", lambda e: e.tensor_tensor(out=C.sz[i2][0:np_, :], in0=C.ps[zb][0:np_, 0:32], in1=C.aconst32[0:np_, 649:681],
                                              op=ALU.add), reads=[C.ps_b[zb], C.const_b], writes=[C.sz_b[i2]])
        P.op("act", lambda e: e.activation(out=C.se[0:np_, :], in_=C.sz[i2][0:np_, :], func=AF.Exp),
             reads=[C.sz_b[i2]], writes=[C.se_b])
        P.op("act", lambda e: e.activation(out=C.ssp[i2][0:np_, :], in_=C.se[0:np_, :], func=AF.Ln, bias=C.onec[0:np_, 0:1], scale=1.0),
             reads=[C.se_b, C.const_b], writes=[C.ssp_b[i2]])
        if diag:
            P.op("dve", lambda e: e.tensor_tensor(out=C.ssp[i2][0:np_, :], in0=C.ssp[i2][0:np_, :], in1=C.mask4[0:np_, :], op=ALU.mult),
                 reads=[C.ssp_b[i2], C.s_small_b], writes=[C.ssp_b[i2]])
        cb = nb((obank, zb))

        def mmc(e):
            ins = e.matmul(C.ps[cb][0:np_, 0:32], lhsT=C.ntri[0:np_, 0:np_], rhs=C.ssp[i2][0:np_, :], start=True, stop=first)
            if not first:
                ins = e.matmul(C.ps[cb][0:np_, 0:32], lhsT=C.nones[:, 0:np_], rhs=C.sspbf[:, :], start=False, stop=True)
            return ins
        P.op("pe", mmc, reads=[C.ssp_b[i2], C.sspbf_b, C.const_b], writes=[C.ps_b[cb]])
        P.op("dve", lambda e: e.tensor_tensor(out=C.sz[i2][0:np_, :], in0=C.sz[i2][0:np_, :], in1=C.ps[cb][0:np_, 0:32], op=ALU.add),
             reads=[C.ps_b[cb], C.sz_b[i2]], writes=[C.sz_b[i2]])
        P.op("act", lambda e: e.activation(out=C.sab[i2][0:np_, :], in_=C.sz[i2][0:np_, :], func=AF.Exp),
             reads=[C.sz_b[i2]], writes=[C.sab_b[i2]])
        if diag:
            P.op("dve", lambda e: e.tensor_tensor(out=C.sab[i2][0:np_, :], in0=C.sab[i2][0:np_, :], in1=C.mask4[0:np_, :], op=ALU.mult),
                 reads=[C.sab_b[i2], C.s_small_b], writes=[C.sab_b[i2]])

        def mmo(e):
            ins = None
            for h in range(8):
                ins = e.matmul(C.ps[obank][:, h * 4:(h + 1) * 4], lhsT=v_of(h), rhs=C.sab[i2][0:np_, h * 4:(h + 1) * 4],
                               start=False, stop=(last and h == 7))
            return ins
        P.op("pe", mmo, reads=[v_b, C.sab_b[i2]], writes=[C.ps_b[obank]])
        if not last:
            P.op("pool", lambda e: e.tensor_tensor(out=C.ssp32[0:np_, :], in0=C.ssp32[0:np_, :], in1=C.ssp[i2][0:np_, :], op=ALU.add),
                 reads=[C.ssp_b[i2], C.ssp32_b], writes=[C.ssp32_b])
            P.op("pool", lambda e: e.tensor_copy(out=C.sspbf[:, :], in_=C.ssp32[:, :]), reads=[C.ssp32_b], writes=[C.sspbf_b])

    unit(0, 4, lambda h: C.skT[:, h * 4:(h + 1) * 4], C.s_small_b, lambda h: C.sv_new[0:4, h * 128:(h + 1) * 128], C.s_small_b,
         True, True, NP == 0)
    for ui, j in enumerate(range(NP - 1, -1, -1)):
        pb = ui % 2
        P.dma("pool", lambda e, pb=pb, j=j: e.indirect_dma_start(
            out=C.kpg32[pb][:, :], out_offset=None, in_=C.ck_rows[:, :],
            in_offset=bass.IndirectOffsetOnAxis(ap=C.pt_i[:, j:j + 1], axis=0)), reads=[C.idx_b], writes=[C.kpg32_b[pb]])
        P.dma("pool", lambda e, pb=pb, j=j: e.indirect_dma_start(
            out=C.vpg32[pb][:, :], out_offset=None, in_=C.cv_rows[:, :],
            in_offset=bass.IndirectOffsetOnAxis(ap=C.pt_i[:, j:j + 1], axis=0)), reads=[C.idx_b], writes=[C.vpg32_b[pb]])
        P.op("act", lambda e, pb=pb: e.activation(out=C.kpg[pb][:, :], in_=C.kpg32[pb][:, :], func=AF.Copy),
             reads=[C.kpg32_b[pb]], writes=[C.kpg_b[pb]])
        P.op("pool", lambda e, pb=pb: e.tensor_copy(out=C.vpg[pb][:, :], in_=C.vpg32[pb][:, :]),
             reads=[C.vpg32_b[pb]], writes=[C.vpg_b[pb]])
        for hq in range(2):
            bank = nb((obank,))
            pbf = C.ps[bank][:, :].bitcast(BF16)

            def tr(e, pb=pb, hq=hq, pbf=pbf):
                ins = None
                for i in range(4):
                    h = hq * 4 + i
                    ins = e.transpose(pbf[:, i * 128:(i + 1) * 128], C.kpg[pb][:, h * 128:(h + 1) * 128], C.ident[:, :])
                return ins
            P.op("pe", tr, reads=[C.kpg_b[pb], C.const_b], writes=[C.ps_b[bank]])
            P.op("dve", lambda e, pb=pb, hq=hq, pbf=pbf: e.tensor_copy(out=C.ktp[pb][:, hq * 512:(hq + 1) * 512], in_=pbf[:, 0:512]),
                 reads=[C.ps_b[bank]], writes=[C.ktp_b[pb]])
        unit(ui + 1, 128, lambda h, pb=pb: C.ktp[pb][:, h * 128:(h + 1) * 128], C.ktp_b[pb],
             lambda h, pb=pb: C.vpg[pb][:, h * 128:(h + 1) * 128], C.vpg_b[pb], False, False, j == 0)
    P.op("act", lambda e: e.activation(
        out=C.mixT[:, 8:16, NB + TOK:NT], in_=C.ps[obank][:, 0:32].rearrange("p (h q) -> p h q", h=8), func=AF.Copy),
        reads=[C.ps_b[obank]], writes=[C.m_b[8 + h][2] for h in range(8)])


PW = NB + TOK


def emit_odd_mixer(P, C):
    emit_norm(P, C, 4, TILES)
    R = C.oR
    Rb = C.oR_b
    l1 = [C.pooled_b[i] for i in range(8)] + [C.m2_b[i] for i in range(8)] + [C.cwg_b, C.hist_b] + Rb
    alias(l1 + [C.t16_b, C.ost_b], C.arena_mixer_bufs + [b for l in C.a_b for b in l] + C.wgu_b + [C.wd_b[1]])
    P.dma("pool", lambda e: e.dma_start(out=C.cwg[:, :], in_=C.cwg_d[:, :]), writes=[C.cwg_b])
    P.dma("sp", lambda e: e.dma_start(out=C.xs_hist[:, :, 0:15], in_=C.pool_hist_d[:, :, :]), writes=[C.hist_b])
    P.dma("sp", lambda e: e.dma_start(out=C.gs[:, :, 0:2], in_=C.conv_hist_d[:, :, :]), writes=[C.hist_b])

    def project(ch, kind, ti, c0, cn):
        bank = C.next_bank()

        def mm(e):
            ins = None
            for kc in range(KC):
                off = (kc * 4 + kind) * 128
                ins = e.matmul(C.ps[bank][:, 0:cn], lhsT=C.wd[0][:, off // 2048, off % 2048:off % 2048 + 128],
                               rhs=C.hT[:, kc, c0:c0 + cn], start=(kc == 0), stop=(kc == KC - 1))
            return ins
        P.op("pe", mm, reads=[C.wd_b[0]] + [C.h_b[kc][ti] for kc in range(KC)], writes=[C.ps_b[bank]])
        return bank

    for ch in range(8):
        g = ch // 2
        w = 2 ** (g + 1)
        P.dma("pool", lambda e, ch=ch: e.dma_start(out=C.wd[0][:, :, :], in_=C.win_odd[ch, :, :, :]), writes=[C.wd_b[0]])
        for ti, (c0, cn) in enumerate(TILES):
            bank = project(ch, 0, ti, c0, cn)
            P.op("act", lambda e, bank=bank, c0=c0, cn=cn: e.activation(out=R[0][:, c0:c0 + cn], in_=C.ps[bank][:, 0:cn], func=AF.Copy),
                 reads=[C.ps_b[bank]], writes=[Rb[0]])
        X = R[0]
        P.op("dve", lambda e, ch=ch: e.tensor_copy(out=C.xs_hist[:, ch, 15:19], in_=X[:, PW:NT]), reads=[Rb[0], C.hist_b], writes=[C.hist_b])
        P.op("dve", lambda e, ch=ch: e.tensor_copy(out=C.ost[:, ch, 0:15], in_=X[:, PW - 15:PW]), reads=[Rb[0]], writes=[C.ost_b])
        P.op("dve", lambda e, ch=ch: e.tensor_copy(out=C.ost[:, ch, 15:30], in_=C.xs_hist[:, ch, 4:19]), reads=[C.hist_b], writes=[C.ost_b])
        src, srcb = X, Rb[0]
        for lv in range(g + 1):
            sh = 2 ** lv
            dst, dstb = (R[1], Rb[1]) if lv % 2 == 0 else (R[2], Rb[2])
            P.op("dve", lambda e, src=src, dst=dst, sh=sh: e.tensor_tensor(
                out=dst[:, sh:PW], in0=src[:, sh:PW], in1=src[:, 0:PW - sh], op=ALU.add), reads=[srcb], writes=[dstb])
            src, srcb = dst, dstb
        P.op("dve", lambda e, src=src, ch=ch, w=w: e.scalar_tensor_tensor(
            out=C.pooled[:, ch, 16:PW], in0=src[:, 16:PW], scalar=1.0 / w, in1=X[:, 16:PW], op0=ALU.mult, op1=ALU.subtract),
            reads=[srcb, Rb[0]], writes=[C.pooled_b[ch]])
        P.op("dve", lambda e, src=src, g=g: e.tensor_tensor(out=C.t16[:, 0:16], in0=src[:, NB:NB + 16], in1=C.oconst[:, 32 + g * 16:32 + (g + 1) * 16],
                                                            op=ALU.mult), reads=[srcb, C.const_b], writes=[C.t16_b])
        P.op("dve", lambda e, ch=ch: e.tensor_tensor(out=C.pooled[:, ch, NB:NB + 16], in0=C.t16[:, 0:16], in1=X[:, NB:NB + 16], op=ALU.subtract),
             reads=[C.t16_b, Rb[0]], writes=[C.pooled_b[ch]])
        hs, hsb = C.xs_hist[:, ch, :], C.hist_b
        cur = hs
        for lv in range(g + 1):
            sh = 2 ** lv
            nxt = C.t16[:, 32 + (lv % 2) * 32:32 + (lv % 2) * 32 + 19]
            P.op("dve", lambda e, cur=cur, nxt=nxt, sh=sh: e.tensor_tensor(out=nxt[:, sh:19], in0=cur[:, sh:19], in1=cur[:, 0:19 - sh], op=ALU.add),
                 reads=[C.hist_b, C.t16_b], writes=[C.t16_b])
            cur = nxt
        P.op("dve", lambda e, cur=cur, ch=ch, w=w, hs=hs: e.scalar_tensor_tensor(
            out=C.pooled[:, ch, PW:NT], in0=cur[:, 15:19], scalar=1.0 / w, in1=hs[:, 15:19], op0=ALU.mult, op1=ALU.subtract),
            reads=[C.t16_b, C.hist_b], writes=[C.pooled_b[ch]])
        for ti, (c0, cn) in enumerate(TILES):
            bgc = project(ch, 2, ti, c0, cn)
            bgh = project(ch, 3, ti, c0, cn)
            bgb = project(ch, 1, ti, c0, cn)
            P.op("act", lambda e, b=bgc, c0=c0, cn=cn: e.activation(out=R[1][:, c0:c0 + cn], in_=C.ps[b][:, 0:cn], func=AF.Copy),
                 reads=[C.ps_b[bgc]], writes=[Rb[1]])
            P.op("dve", lambda e, b=bgh, c0=c0, cn=cn: e.tensor_tensor(out=R[2][:, c0:c0 + cn], in0=R[1][:, c0:c0 + cn], in1=C.ps[b][:, 0:cn], op=ALU.mult),
                 reads=[C.ps_b[bgh], Rb[1]], writes=[Rb[2]])
            P.op("act", lambda e, b=bgb, c0=c0, cn=cn: e.activation(out=R[0][:, c0:c0 + cn], in_=C.ps[b][:, 0:cn], func=AF.Copy),
                 reads=[C.ps_b[bgb]], writes=[Rb[0]])
        Gm, GB = R[2], R[0]
        P.op("dve", lambda e, ch=ch: e.tensor_copy(out=C.gs[:, ch, 2:6], in_=Gm[:, PW:NT]), reads=[Rb[2], C.hist_b], writes=[C.hist_b])
        P.op("dve", lambda e, ch=ch: e.tensor_copy(out=C.ost[:, ch, 30:32], in_=Gm[:, PW - 2:PW]), reads=[Rb[2]], writes=[C.ost_b])
        P.op("dve", lambda e, ch=ch: e.tensor_copy(out=C.ost[:, ch, 32:34], in_=C.gs[:, ch, 4:6]), reads=[C.hist_b], writes=[C.ost_b])
        w0, w1, w2 = (C.oconst[:, 8 + ch * 3 + tap:8 + ch * 3 + tap + 1] for tap in range(3))
        T = R[1]
        P.op("dve", lambda e, w2=w2: e.tensor_scalar(out=T[:, 2:PW], in0=Gm[:, 2:PW], scalar1=w2, scalar2=0.0, op0=ALU.mult, op1=ALU.add),
             reads=[Rb[2], C.const_b], writes=[Rb[1]])
        P.op("dve", lambda e, w1=w1: e.scalar_tensor_tensor(out=T[:, 2:PW], in0=Gm[:, 1:PW - 1], scalar=w1, in1=T[:, 2:PW], op0=ALU.mult, op1=ALU.add),
             reads=[Rb[2], Rb[1], C.const_b], writes=[Rb[1]])
        P.op("dve", lambda e, w0=w0: e.scalar_tensor_tensor(out=T[:, 2:PW], in0=Gm[:, 0:PW - 2], scalar=w0, in1=T[:, 2:PW], op0=ALU.mult, op1=ALU.add),
             reads=[Rb[2], Rb[1], C.const_b], writes=[Rb[1]])
        P.op("dve", lambda e, ch=ch: e.tensor_tensor(out=C.mix2[:, ch, 2:PW], in0=T[:, 2:PW], in1=GB[:, 2:PW], op=ALU.mult),
             reads=[Rb[1], Rb[0]], writes=[C.m2_b[ch]])
        P.op("dve", lambda e, ch=ch: e.memset(C.mix2[:, ch, 0:2], 0.0), writes=[C.m2_b[ch]])
        gsr = C.gs[:, ch, :]
        ts = C.t16[:, 96:100]
        P.op("dve", lambda e, gsr=gsr, ts=ts, w2=w2: e.tensor_scalar(out=ts, in0=gsr[:, 2:6], scalar1=w2, scalar2=0.0, op0=ALU.mult, op1=ALU.add),
             reads=[C.hist_b, C.const_b], writes=[C.t16_b])
        P.op("dve", lambda e, gsr=gsr, ts=ts, w1=w1: e.scalar_tensor_tensor(out=ts, in0=gsr[:, 1:5], scalar=w1, in1=ts, op0=ALU.mult, op1=ALU.add),
             reads=[C.hist_b, C.t16_b, C.const_b], writes=[C.t16_b])
        P.op("dve", lambda e, gsr=gsr, ts=ts, w0=w0: e.scalar_tensor_tensor(out=ts, in0=gsr[:, 0:4], scalar=w0, in1=ts, op0=ALU.mult, op1=ALU.add),
             reads=[C.hist_b, C.t16_b, C.const_b], writes=[C.t16_b])
        P.op("dve", lambda e, ts=ts, ch=ch: e.tensor_tensor(out=C.mix2[:, ch, PW:NT], in0=ts, in1=GB[:, PW:NT], op=ALU.mult),
             reads=[C.t16_b, Rb[0]], writes=[C.m2_b[ch]])
    for ch in range(8):
        P.op("dve", lambda e, ch=ch: e.memset(C.pooled[:, ch, 0:16], 0.0), writes=[C.pooled_b[ch]])
    P.dma("sp", lambda e: e.dma_start(out=C.ost_d[:, :, :], in_=C.ost[:, :, :]), reads=[C.ost_b], is_output=True)
    alias([b for l in C.m_b for b in l], [b for l in C.h_b for b in l])
    for g in range(4):
        for oc in range(2):
            for ti, (c0, cn) in enumerate(TILES):
                bank = C.next_bank()

                def mm(e, g=g, oc=oc, bank=bank, c0=c0, cn=cn):
                    ins = None
                    for ic in range(2):
                        off = ((g * 2 + ic) * 2 + oc) * 128
                        ins = e.matmul(C.ps[bank][:, 0:cn], lhsT=C.cwg[:, off:off + 128], rhs=C.pooled[:, g * 2 + ic, c0:c0 + cn],
                                       start=(ic == 0), stop=(ic == 1))
                    return ins
                P.op("pe", mm, reads=[C.cwg_b, C.pooled_b[g * 2], C.pooled_b[g * 2 + 1]], writes=[C.ps_b[bank]])
                P.op("act", lambda e, g=g, oc=oc, bank=bank, c0=c0, cn=cn: e.activation(
                    out=C.mixT[:, g * 2 + oc, c0:c0 + cn], in_=C.ps[bank][:, 0:cn], func=AF.Copy,
                    scale=C.oconst[:, g * 2 + oc:g * 2 + oc + 1]), reads=[C.ps_b[bank], C.const_b], writes=[C.m_b[g * 2 + oc][ti]])


def emit_even_out(P, C, wout=None, hi=None, hi_b=None):
    wout = C.wout if wout is None else wout
    mixc = (lambda fc: C.mixT[:, fc, :]) if hi is None else (lambda fc: C.mixT[:, fc, :] if fc < 8 else hi[:, fc - 8, :])
    mixb = (lambda fc, ti: C.m_b[fc][ti]) if hi is None else (lambda fc, ti: C.m_b[fc][ti] if fc < 8 else hi_b[fc - 8])
    if _DEV["no_attn"] and hi is None:
        for fc in range(8, 16):
            for ti, (c0, cn) in enumerate(TILES):
                P.op("dve", lambda e, fc=fc, c0=c0, cn=cn: e.memset(C.mixT[:, fc, c0:c0 + cn], 0.0),
                     writes=[C.m_b[fc][ti]])
    for dq in range(4):
        slot = dq % 2
        P.dma("pool", lambda e, dq=dq, slot=slot: e.dma_start(out=C.wd[slot][:, :, :], in_=wout[dq, :, :, :]),
              writes=[C.wd_b[slot]])
        for dl in range(4):
            dc = dq * 4 + dl
            for ti, (c0, cn) in enumerate(TILES):
                bank = C.next_bank()

                def mm(e, slot=slot, dl=dl, bank=bank, c0=c0, cn=cn):
                    ins = None
                    for fc in range(KC):
                        off = fc * 512 + dl * 128
                        ins = e.matmul(C.ps[bank][:, 0:cn], lhsT=C.wd[slot][:, off // 2048, off % 2048:off % 2048 + 128],
                                       rhs=mixc(fc)[:, c0:c0 + cn], start=(fc == 0), stop=(fc == KC - 1))
                    return ins
                P.op("pe", mm, reads=[C.wd_b[slot]] + [mixb(fc, ti) for fc in range(KC)], writes=[C.ps_b[bank]])
                P.op("dve", lambda e, dc=dc, bank=bank, c0=c0, cn=cn: e.tensor_tensor(
                    out=C.xT[:, dc, c0:c0 + cn], in0=C.ps[bank][:, 0:cn], in1=C.xT[:, dc, c0:c0 + cn], op=ALU.add),
                    reads=[C.ps_b[bank], C.x_b[dc][ti]], writes=[C.x_b[dc][ti]])


def build_program(n_phys=1280, n_pages=128):
    import contextlib
    nc = bass.Bass("TRN2", target_bir_lowering=False)
    stack = contextlib.ExitStack()
    C = Ctx()
    P = Prog()
    xT_d = nc.dram_tensor("xT", [128, KC, NT], F32, kind="ExternalInput").ap()
    gains_d = nc.dram_tensor("gains", [128, 6, KC], F32, kind="ExternalInput").ap()
    if not _DEV["skip_ffn"]:
        C.wgu_dram = nc.dram_tensor("wgu", [4, FC, 128, KC * 2 * 128], F32, kind="ExternalInput").ap()
        C.wd_dram = nc.dram_tensor("wd", [4, NG, 128, G, D], F32, kind="ExternalInput").ap()
    yT_d = nc.dram_tensor("yT", [128, KC, NT], F32, kind="ExternalOutput").ap()
    C.win_tm = nc.dram_tensor("win_tm", [8, 128, 4, 2048], F32, kind="ExternalInput").ap()
    grep_d = nc.dram_tensor("g_rep", [128, 128 + 128 + 1024], F32, kind="ExternalInput").ap()
    C.wau = nc.dram_tensor("wau", [2, 128, 4, 2048], F32, kind="ExternalInput").ap()
    C.wout = nc.dram_tensor("wout", [4, 128, 4, 2048], F32, kind="ExternalInput").ap()
    gconst_d = nc.dram_tensor("gconst", [128, 8 * 128 + 128], F32, kind="ExternalInput").ap()
    C.wsT_d = nc.dram_tensor("wsT", [128, 8 * 128], F32, kind="ExternalInput").ap()
    C.qscr = nc.dram_tensor("qscr", [NT, 1024], F32).ap()
    C.kscr = nc.dram_tensor("kscr", [896 + NT, 1024], F32).ap()
    C.vscr = nc.dram_tensor("vscr", [896 + NT, 1024], F32).ap()
    aconst_d = nc.dram_tensor("aconst", [128, ACW], F32, kind="ExternalInput").ap()
    xo_d = nc.dram_tensor("xoT", [128, KC, OTHER_ROWS], F32, kind="ExternalInput").ap()
    C.ck_rows = nc.dram_tensor("ck_rows", [n_phys * 128, 1024], F32, kind="ExternalInput").ap()
    C.cv_rows = nc.dram_tensor("cv_rows", [n_phys * 128, 1024], F32, kind="ExternalInput").ap()
    C.pt_d = nc.dram_tensor("pt_rep", [128, n_pages], I32, kind="ExternalInput").ap()
    C.n_pages = n_pages
    C.win_odd = nc.dram_tensor("win_odd", [8, 128, 4, 2048], F32, kind="ExternalInput").ap()
    C.wout_odd = nc.dram_tensor("wout_odd", [4, 128, 4, 2048], F32, kind="ExternalInput").ap()
    C.cwg_d = nc.dram_tensor("cwg", [128, 2048], F32, kind="ExternalInput").ap()
    oconst_d = nc.dram_tensor("oconst", [128, 96], F32, kind="ExternalInput").ap()
    C.pool_hist_d = nc.dram_tensor("pool_hist", [128, 8, 15], F32, kind="ExternalInput").ap()
    C.conv_hist_d = nc.dram_tensor("conv_hist", [128, 8, 2], F32, kind="ExternalInput").ap()
    C.ost_d = nc.dram_tensor("ost", [128, 8, 34], F32, kind="ExternalOutput").ap()
    C.k_out = nc.dram_tensor("k_out", [NT, 1024], F32, kind="ExternalOutput").ap()
    C.v_out = nc.dram_tensor("v_out", [NT, 1024], F32, kind="ExternalOutput").ap()
    C.avn_out = nc.dram_tensor("avn_out", [4, 1024], F32, kind="ExternalOutput").ap()

    def sb(name, shape, dt):
        return stack.enter_context(nc.sbuf_tensor(name, shape, dt))
    C.xT = sb("xT_sb", [128, KC, NT], F32)
    C.hT = sb("hT_sb", [128, KC, NT], BF16)
    C.gains = sb("gains_sb", [128, 6, KC], F32)
    C.ones32 = sb("ones32", [128, 128], F32)
    C.epsc = sb("epsc", [128, 1], F32)
    C.sq = [sb("sq%d" % i, [128, 512], F32) for i in range(2)]
    C.rstd = sb("rstd", [128, 512], F32)
    C.sg = [sb("sg%d" % i, [128, 512], BF16) for i in range(3)]
    A_AT = G * NT
    ARENA = 2 * A_AT + 3 * 4096
    C.arena = sb("arena", [128, ARENA], BF16)
    C.aT = [C.arena[:, i * A_AT:(i + 1) * A_AT].rearrange("p (g t) -> p g t", g=G) for i in range(2)]
    C.wgu = [C.arena[:, 2 * A_AT + i * 4096:2 * A_AT + (i + 1) * 4096] for i in range(3)]
    a32 = C.arena[:, :].bitcast(F32)
    C.avn = C.arena[:, 0:10240].rearrange("p (b f) -> p b f", b=10)
    C.wmT = C.arena[:, 10240:11264].rearrange("p (g c) -> p g c", g=8)
    C.stage = [a32[:, 5632 + i * 512:5632 + (i + 1) * 512] for i in range(3)]
    C.gel = a32[:, 7168:8192]
    C.auT = C.arena[:, 11264:11264 + 8 * NT].rearrange("p (g t) -> p g t", g=8)
    C.g_rep = sb("g_rep_sb", [128, 128 + 128 + 1024], F32)
    C.gq, C.gk, C.gav = C.g_rep[:, 0:128], C.g_rep[:, 128:256], C.g_rep[:, 256:1280]
    C.ssq = sb("ssq", [128, 8], F32)
    C.gconst = sb("gconst_sb", [128, 8 * 128 + 128], F32)
    C.bs_bc = C.gconst[:, 0:1024]
    C.mixT = C.hT
    C.au_b = [Buf("au%d" % i) for i in range(8)]
    C.wmT_b = Buf("wmT")
    C.qscr_b = Buf("qscr")
    C.kscr_b, C.vscr_b = Buf("kscr"), Buf("vscr")
    C.aconst32 = sb("aconst32", [128, ACW], F32)
    C.aconst = sb("aconst_bf", [128, 5 * 128], BF16)
    C.ident, C.tris, C.ntri, C.nones, C.zerosb = (C.aconst[:, i * 128:(i + 1) * 128] for i in range(5))
    C.sbias = C.aconst32[:, 640:648]
    C.onec = sb("onec", [128, 1], F32)
    C.kh = C.arena[:, 0:2048].rearrange("p (b d) -> p b d", b=16)
    C.vh = C.arena[:, 2048:4096].rearrange("p (b d) -> p b d", b=16)
    C.qh = C.arena[:, 4096:5248].rearrange("p (b d) -> p b d", b=9)
    C.khT = C.arena[:, 5248:7296]
    C.qhT = C.arena[:, 7296:8448]
    C.ebuf = [a32[:, 4224 + i * 384:4224 + (i + 1) * 384] for i in range(2)]
    C.sp32 = a32[:, 4992:5376]
    C.spb = [C.arena[:, 10752 + i * 384:10752 + (i + 1) * 384] for i in range(2)]
    C.abuf = [C.arena[:, 11520 + i * 384:11520 + (i + 1) * 384] for i in range(2)]
    C.spbf = C.arena[:, 12288:12672]
    C.kh_b, C.vh_b, C.qh_b, C.khT_b, C.qhT_b = (Buf(n) for n in ("kh", "vh", "qh", "khT", "qhT"))
    ai32 = C.arena[:, :].bitcast(I32)
    C.kpg32 = [a32[:, i * 1024:(i + 1) * 1024] for i in range(2)]
    C.vpg32 = [a32[:, 2048 + i * 1024:2048 + (i + 1) * 1024] for i in range(2)]
    C.kpg = [C.arena[:, 8192 + i * 1024:8192 + (i + 1) * 1024] for i in range(2)]
    C.vpg = [C.arena[:, 10240 + i * 1024:10240 + (i + 1) * 1024] for i in range(2)]
    C.ktp = [C.arena[:, 12288 + i * 1024:12288 + (i + 1) * 1024] for i in range(2)]
    C.sq_new, C.sk_new, C.sv_new = (C.arena[:, 14336 + i * 1024:14336 + (i + 1) * 1024] for i in range(3))
    C.sqT, C.skT, C.mask4 = (C.arena[:, 17408 + i * 32:17408 + (i + 1) * 32] for i in range(3))
    C.sz = [a32[:, 8752 + i * 32:8752 + (i + 1) * 32] for i in range(2)]
    C.se = a32[:, 8816:8848]
    C.ssp = [C.arena[:, 17696 + i * 32:17696 + (i + 1) * 32] for i in range(2)]
    C.sab = [C.arena[:, 17760 + i * 32:17760 + (i + 1) * 32] for i in range(2)]
    C.ssp32 = a32[:, 8912:8944]
    C.sspbf = C.arena[:, 17888:17920]
    C.pt_i = ai32[:, 8960:8960 + n_pages]
    C.pt_f = a32[:, 8960 + n_pages:8960 + 2 * n_pages]
    C.kpg32_b, C.vpg32_b, C.kpg_b, C.vpg_b, C.ktp_b = ([Buf("%s%d" % (n, i)) for i in range(2)] for n in ("kpg32", "vpg32", "kpg", "vpg", "ktp"))
    C.s_small_b, C.se_b, C.ssp32_b, C.sspbf_b, C.idx_b = (Buf(n) for n in ("ssmall", "se", "ssp32", "sspbf", "idx"))
    C.sz_b, C.ssp_b, C.sab_b = ([Buf("%s%d" % (n, i)) for i in range(2)] for n in ("sz", "ssp", "sab"))
    C.e_b = [Buf("e0"), Buf("e1")]
    C.sp_b = [Buf("sp0"), Buf("sp1")]
    C.a_b2 = [Buf("ab0"), Buf("ab1")]
    C.sp32_b, C.spbf_b = Buf("sp32"), Buf("spbf")
    C.m_b = [[Buf("m%d_%d" % (fc, t)) for t in range(len(TILES))] for fc in range(KC)]
    C.stage_b = [Buf("stage%d" % i) for i in range(3)]
    C.gel_b = Buf("gel")
    C.avn_b = [Buf("avn%d" % i) for i in range(10)]
    C.ssq_b = Buf("ssq")
    C.mix_b_all = C.stage_b + [C.gel_b] + C.avn_b
    C.mix_b_all2 = None
    C.stage_next = 0
    C.wd = [sb("wd%d" % i, [128, G, D], BF16) for i in range(2)]
    C.ps = [stack.enter_context(nc.psum_tensor("ps%d" % i, [128, 512], F32)) for i in range(8)]
    C.x_b = [[Buf("x%d_%d" % (kc, t)) for t in range(len(TILES))] for kc in range(KC)]
    C.h_b = [[Buf("h%d_%d" % (kc, t)) for t in range(len(TILES))] for kc in range(KC)]
    C.sq_b = [Buf("sq0"), Buf("sq1")]
    C.rstd_b = Buf("rstd")
    C.sg_b = [Buf("sg%d" % i) for i in range(3)]
    C.a_b = [[Buf("a%d_%d" % (i, f)) for f in range(G)] for i in range(2)]
    C.wgu_b = [Buf("wgu%d" % i) for i in range(3)]
    C.wd_b = [Buf("wd%d" % i) for i in range(2)]
    C.ps_b = [Buf("ps%d" % i) for i in range(8)]
    C.const_b = Buf("const")
    C.wgu_next = 0
    C.sg_next = 0
    C._bank = 0

    def next_bank():
        b = C._bank
        C._bank = (b + 1) % 8
        return b
    C.next_bank = next_bank

    C.pooled = C.arena[:, 0:8 * NT].rearrange("p (g t) -> p g t", g=8)
    C.mix2 = C.arena[:, 8 * NT:16 * NT].rearrange("p (g t) -> p g t", g=8)
    C.cwg = C.arena[:, 16 * NT:16 * NT + 2048]
    o32 = 8 * NT + 1024
    C.xs_hist = a32[:, o32:o32 + 152].rearrange("p (g t) -> p g t", g=8)
    C.gs = a32[:, o32 + 152:o32 + 200].rearrange("p (g t) -> p g t", g=8)
    wd1_32 = C.wd[1][:, :, :].rearrange("p a b -> p (a b)").bitcast(F32)
    C.oR = [wd1_32[:, i * NT:(i + 1) * NT] for i in range(3)]
    C.t16 = wd1_32[:, 3 * NT:3 * NT + 128]
    C.ost = wd1_32[:, 3 * NT + 128:3 * NT + 128 + 272].rearrange("p (g t) -> p g t", g=8)
    C.oconst = sb("oconst_sb", [128, 96], F32)
    C.pooled_b = [Buf("pooled%d" % i) for i in range(8)]
    C.m2_b = [Buf("m2_%d" % i) for i in range(8)]
    C.cwg_b, C.hist_b, C.t16_b, C.ost_b = (Buf(n) for n in ("cwg", "hist", "t16", "ost"))
    C.oR_b = [Buf("oR%d" % i) for i in range(3)]
    C.l1_arena_bufs = C.pooled_b + C.m2_b + [C.cwg_b, C.hist_b]
    C.arena_mixer_bufs_wo_sample = (C.stage_b + [C.gel_b] + C.avn_b + C.au_b + [C.wmT_b, C.kh_b, C.vh_b, C.qh_b, C.khT_b, C.qhT_b]
                                    + C.e_b + C.sp_b + C.a_b2 + [C.sp32_b, C.spbf_b])
    C.arena_mixer_bufs = (C.arena_mixer_bufs_wo_sample + C.kpg32_b + C.vpg32_b + C.kpg_b + C.vpg_b + C.ktp_b
                          + [C.s_small_b, C.se_b, C.ssp32_b, C.sspbf_b, C.idx_b] + C.sz_b + C.ssp_b + C.sab_b)
    all_x = [C.x_b[kc][t] for kc in range(KC) for t in range(len(TILES))]
    P.dma("sp", lambda e: e.dma_start(out=C.gains[:, :, :], in_=gains_d[:, :, :]), writes=[C.const_b])
    P.dma("sp", lambda e: e.dma_start(out=C.g_rep[:, :], in_=grep_d[:, :]), writes=[C.const_b])
    P.dma("sp", lambda e: e.dma_start(out=C.gconst[:, :], in_=gconst_d[:, :]), writes=[C.const_b])
    P.dma("sp", lambda e: e.dma_start(out=C.aconst32[:, :], in_=aconst_d[:, :]), writes=[C.const_b])
    P.op("dve", lambda e: e.tensor_copy(out=C.aconst[:, :], in_=C.aconst32[:, 0:640]), reads=[C.const_b], writes=[C.const_b])
    P.op("dve", lambda e: e.memset(C.onec[:, :], 1.0), writes=[C.const_b])
    P.dma("sp", lambda e: e.dma_start(out=C.oconst[:, :], in_=oconst_d[:, :]), writes=[C.const_b])
    P.op("dve", lambda e: e.memset(C.ones32[:, :], 1.0 / D), writes=[C.const_b])
    P.op("dve", lambda e: e.memset(C.epsc[:, :], EPS), writes=[C.const_b])
    TILES0 = [(0, 512), (512, OTHER_ROWS - 512)]
    for ti, (c0, cn) in enumerate(TILES0):
        for kh in range(2):
            P.dma("sp", lambda e, c0=c0, cn=cn, kh=kh: e.dma_start(
                out=C.xT[:, kh * 8:(kh + 1) * 8, c0:c0 + cn], in_=xo_d[:, kh * 8:(kh + 1) * 8, c0:c0 + cn]),
                writes=[C.x_b[kc][ti] for kc in range(kh * 8, (kh + 1) * 8)])
    if not _DEV["skip_ffn"]:
        emit_ffn(P, C, 0, 0, TILES0)
    if not _DEV["only_l1"]:
        emit_even_proj(P, C, tiles=TILES0, only_kv_blocks=[(i * 128, 128) for i in range(OTHER_ROWS // 128)])
    for ti, (c0, cn) in enumerate(TILES):
        for kh in range(2):
            P.dma("sp", lambda e, c0=c0, cn=cn, kh=kh: e.dma_start(
                out=C.xT[:, kh * 8:(kh + 1) * 8, c0:c0 + cn], in_=xT_d[:, kh * 8:(kh + 1) * 8, c0:c0 + cn]),
                writes=[C.x_b[kc][ti] for kc in range(kh * 8, (kh + 1) * 8)])
    for b in all_x:
        pass

    if not _DEV["skip_ffn"]:
        emit_ffn(P, C, 0, 0, TILES)
    if not _DEV["only_l1"]:
        emit_even_proj(P, C)
        emit_even_gate(P, C)
    if not _DEV["no_attn"] and not _DEV["only_l1"]:
        emit_attn_prompt(P, C)
        if not _DEV["no_sample"]:
            emit_attn_sample(P, C)
    if not _DEV["only_l1"]:
        emit_even_out(P, C)
    if not _DEV["skip_ffn"]:
        emit_ffn(P, C, 1, 2, TILES)
        emit_ffn(P, C, 2, 3, TILES)
    emit_odd_mixer(P, C)
    emit_even_out(P, C, wout=C.wout_odd, hi=C.mix2, hi_b=C.m2_b)
    if not _DEV["skip_ffn"]:
        emit_ffn(P, C, 3, 5, TILES)

    for ti, (c0, cn) in enumerate(TILES):
        for kh in range(2):
            P.dma("sp", lambda e, c0=c0, cn=cn, kh=kh: e.dma_start(
                out=yT_d[:, kh * 8:(kh + 1) * 8, c0:c0 + cn], in_=C.xT[:, kh * 8:(kh + 1) * 8, c0:c0 + cn]),
                reads=[C.x_b[kc][ti] for kc in range(kh * 8, (kh + 1) * 8)], is_output=True)
    P.emit(nc, stack)
    stack.close()
    return nc


_NC_CACHE = {}


def _host_layouts(inp):
    f32 = np.float32
    ng = np.asarray(inp["norm_g"], f32)
    gains = np.ascontiguousarray(ng.reshape(6, KC, 128).transpose(2, 0, 1))
    wg = np.asarray(inp["ffn_w_gate"], f32).reshape(4, KC, 128, FC, 128)
    wu = np.asarray(inp["ffn_w_up"], f32).reshape(4, KC, 128, FC, 128)
    wgu = np.stack([wg, wu], axis=4)
    wgu = np.ascontiguousarray(wgu.transpose(0, 3, 2, 1, 4, 5)).reshape(4, FC, 128, KC * 2 * 128)
    wd = np.asarray(inp["ffn_w_down"], f32).reshape(4, NG, G, 128, D)
    wd = np.ascontiguousarray(wd.transpose(0, 1, 3, 2, 4))
    wi = np.asarray(inp["w_in_even"], f32)[0]
    secs = []
    for c0 in (0, 512, 1024, 1536, 2048, 2560, 4096, 4608):
        w = wi[:, c0:c0 + 512].reshape(KC, 128, 512).transpose(1, 0, 2)
        secs.append(w.reshape(128, 4, 2048))
    win_tm = np.ascontiguousarray(np.stack(secs, 0))
    g_rep = np.concatenate([np.tile(np.asarray(inp["q_norm_g"], f32)[0][None, :], (128, 1)),
                            np.tile(np.asarray(inp["k_norm_g"], f32)[0][None, :], (128, 1)),
                            np.tile(np.asarray(inp["a_v_norm_g"], f32)[0][None, :], (128, 1))], axis=1)
    au = wi[:, 3072:4096].reshape(KC, 128, 2, 4, 128).transpose(2, 1, 0, 3, 4)
    wau = np.ascontiguousarray(au).reshape(2, 128, 4, 2048)
    wo = np.asarray(inp["w_out_even"], f32)[0].reshape(KC, 128, 4, 512).transpose(2, 1, 0, 3)
    wout = np.ascontiguousarray(wo).reshape(4, 128, 4, 2048)
    wsT = np.asarray(inp["a_ws"], f32)[0].transpose(2, 0, 1).reshape(128, 1024)
    bs_rep = np.tile(np.asarray(inp["a_bs"], f32)[0].reshape(1, 1024), (128, 1))
    tri = (np.arange(128)[:, None] <= np.arange(128)[None, :]).astype(f32)
    gconst = np.ascontiguousarray(np.concatenate([bs_rep, tri], axis=1))
    wsT = np.ascontiguousarray(wsT)
    wo_ = np.asarray(inp["w_in_odd"], f32)[0].reshape(KC, 128, 4, 8, 128).transpose(3, 1, 0, 2, 4)
    win_odd = np.ascontiguousarray(wo_).reshape(8, 128, 4, 2048)
    woo = np.asarray(inp["w_out_odd"], f32)[0].reshape(KC, 128, 4, 512).transpose(2, 1, 0, 3)
    wout_odd = np.ascontiguousarray(woo).reshape(4, 128, 4, 2048)
    cw = np.asarray(inp["c_wg"], f32)[0].reshape(4, 2, 128, 2, 128).transpose(2, 0, 1, 3, 4)
    cwg = np.ascontiguousarray(cw).reshape(128, 2048)
    csc = np.asarray(inp["c_scale"], f32)[0].reshape(8, 128).T
    taps = np.asarray(inp["d_conv_w"], f32)[0].reshape(3, 8, 128).transpose(2, 1, 0).reshape(128, 24)
    odd_pack = (win_odd, wout_odd, cwg, np.ascontiguousarray(np.concatenate([csc, taps], axis=1)))
    ii = np.arange(128)
    aconst = np.concatenate([np.eye(128, dtype=f32), (ii[:, None] < ii[None, :]).astype(f32),
                             -(ii[:, None] >= ii[None, :]).astype(f32), -np.ones((128, 128), f32),
                             np.zeros((128, 128), f32),
                             np.tile(np.asarray(inp["sb_bias"], f32)[0][None, :], (128, 1)),
                             ii[:, None].astype(f32),
                             np.tile(np.repeat(np.asarray(inp["sb_bias"], f32)[0], 4)[None, :], (128, 1))], axis=1)
    return gains, wgu, wd, win_tm, np.ascontiguousarray(g_rep), wau, wout, gconst, np.ascontiguousarray(aconst), wsT, odd_pack


def kernel(**inp):
    f32 = np.float32
    xp = np.asarray(inp["x_prompt"], f32)
    xs = np.asarray(inp["x_sample"], f32)
    gains, wgu, wd, win_tm, g_rep, wau, wout, gconst, aconst, wsT, odd_pack = _host_layouts(inp)
    win_odd, wout_odd, cwg, oc32 = odd_pack
    ck = np.asarray(inp["cache_k"], f32)
    n_phys, n_pages = ck.shape[1], np.asarray(inp["page_table"]).shape[1]
    ck_rows = ck.reshape(n_phys * 128, 1024)
    cv_rows = np.asarray(inp["cache_v"], f32).reshape(n_phys * 128, 1024)
    in_maps = []
    for c in range(N_CORES):
        b, half = c // 2, c % 2
        start = half * TOK
        xt = np.zeros((NT, D), f32)
        if start >= NB:
            xt[0:NB] = xp[b, start - NB:start]
        xt[NB:NB + TOK] = xp[b, start:start + TOK]
        xt[NB + TOK:] = xs[c]
        xT = np.ascontiguousarray(xt.reshape(NT, KC, 128).transpose(2, 1, 0))
        xo = np.zeros((OTHER_ROWS, D), f32)
        if half == 1:
            xo[:] = xp[b, 0:OTHER_ROWS]
        xoT = np.ascontiguousarray(xo.reshape(OTHER_ROWS, KC, 128).transpose(2, 1, 0))
        wins = np.array([2, 4, 8, 16], f32)
        pos = (half * TOK + np.arange(16, dtype=f32))[None, :] + 1.0
        invc = np.tile((1.0 / np.minimum(pos, wins[:, None])).reshape(1, 64).astype(f32), (128, 1))
        oconst = np.ascontiguousarray(np.concatenate([oc32, invc], axis=1))
        pool_hist = np.ascontiguousarray(np.asarray(inp["state_pool"], f32)[0, c].reshape(15, 8, 128).transpose(2, 1, 0))
        conv_hist = np.ascontiguousarray(np.asarray(inp["state_conv"], f32)[0, c].reshape(2, 8, 128).transpose(2, 1, 0))
        pt_rep = np.ascontiguousarray(np.tile(np.asarray(inp["page_table"], np.int32)[c][None, :], (128, 1)))
        m = {"xT": xT, "gains": gains, "win_tm": win_tm, "g_rep": g_rep, "wau": wau, "wout": wout, "gconst": gconst, "aconst": aconst, "wsT": wsT,
             "xoT": xoT, "ck_rows": ck_rows, "cv_rows": cv_rows, "pt_rep": pt_rep, "win_odd": win_odd, "wout_odd": wout_odd,
             "cwg": cwg, "oconst": oconst, "pool_hist": pool_hist, "conv_hist": conv_hist}
        if not _DEV["skip_ffn"]:
            m.update({"wgu": wgu, "wd": wd})
        in_maps.append(m)
    if "nc" not in _NC_CACHE:
        _NC_CACHE["nc"] = build_program(n_phys, n_pages)
    res = run_bass_kernel_spmd(_NC_CACHE["nc"], in_maps, core_ids=list(range(N_CORES)))
    B, S = 4, 2048
    y_prompt = np.zeros((B, S, D), f32)
    y_sample = np.zeros((8, 4, D), f32)
    for c in range(N_CORES):
        b, half = c // 2, c % 2
        yT = np.asarray(res.results[c]["yT"])
        y = yT.transpose(2, 1, 0).reshape(NT, D)
        y_prompt[b, half * TOK:(half + 1) * TOK] = y[NB:NB + TOK]
        y_sample[c] = y[NB + TOK:]
    z = lambda *s: np.zeros(s, f32)
    nk_p, nv_p = z(1, 4, 2048, 8, 128), z(1, 4, 2048, 8, 128)
    nk_s, nv_s, nav_s = z(1, 8, 4, 8, 128), z(1, 8, 4, 8, 128), z(1, 8, 4, 1024)
    for c in range(N_CORES):
        b, half = c // 2, c % 2
        r = res.results[c]
        ko, vo = np.asarray(r["k_out"]), np.asarray(r["v_out"])
        nk_p[0, b, half * TOK:(half + 1) * TOK] = ko[NB:NB + TOK].reshape(TOK, 8, 128)
        nv_p[0, b, half * TOK:(half + 1) * TOK] = vo[NB:NB + TOK].reshape(TOK, 8, 128)
        nk_s[0, c] = ko[NB + TOK:].reshape(4, 8, 128)
        nv_s[0, c] = vo[NB + TOK:].reshape(4, 8, 128)
        nav_s[0, c] = np.asarray(r["avn_out"])
    pool_p, pool_s, conv_p, conv_s = z(1, 4, 15, 1024), z(1, 8, 15, 1024), z(1, 4, 2, 1024), z(1, 8, 2, 1024)
    for c in range(N_CORES):
        b, half = c // 2, c % 2
        ost = np.asarray(res.results[c]["ost"])
        tm = lambda a: a.transpose(2, 1, 0).reshape(a.shape[2], 1024)
        if half == 1:
            pool_p[0, b] = tm(ost[:, :, 0:15])
            conv_p[0, b] = tm(ost[:, :, 30:32])
        pool_s[0, c] = tm(ost[:, :, 15:30])
        conv_s[0, c] = tm(ost[:, :, 32:34])
    return (y_prompt, y_sample, nk_p, nv_p, nk_s, nv_s, nav_s, pool_p, pool_s, conv_p, conv_s)
```

```python
import numpy as np
import concourse.bass as bass
import concourse.mybir as mybir
from concourse.bass_utils import run_bass_kernel_spmd

F32 = mybir.dt.float32
BF16 = mybir.dt.bfloat16
I32 = mybir.dt.int32
AF = mybir.ActivationFunctionType
ALU = mybir.AluOpType

N_CORES = 8
_DEV = {"only_l0": False, "only_l1": False, "skip_ffn": False, "no_attn": False, "max_pages": None, "no_sample": False}
D = 2048
KC = D // 128
FFN = 5632
FC = FFN // 128
TOK = 1024
TT = 512
NTT = TOK // TT
EPS = 1e-6
G = 4
NG = FC // G


class Buf:
    __slots__ = ("name", "w", "r")

    def __init__(self, name):
        self.name = name
        self.w = None
        self.r = {}


class Prog:
    COMPUTE = ("pe", "act", "dve", "pool")
    NRING = 12

    def __init__(self):
        self.ops = {e: [] for e in ("pe", "act", "dve", "pool", "sp")}
        self.cnt = {e: 0 for e in self.COMPUTE}
        self.ring = {"pool": [0] * self.NRING, "sp": [0] * self.NRING, "act": [0] * self.NRING}
        self.ring_next = {"pool": 0, "sp": 0, "act": 0}
        self.out_events = []

    def _deps(self, reads, writes):
        deps = {}

        def add(ev):
            if ev is not None and deps.get(ev[0], 0) < ev[1]:
                deps[ev[0]] = ev[1]
        for b in reads:
            add(b.w)
        for b in writes:
            add(b.w)
            for k, v in b.r.items():
                add((k, v))
        return deps

    def _commit(self, ev, reads, writes):
        for b in reads:
            if b.r.get(ev[0], 0) < ev[1]:
                b.r[ev[0]] = ev[1]
        for b in writes:
            b.w = ev
            b.r = {}

    def op(self, eng, fn, reads=(), writes=()):
        deps = self._deps(reads, writes)
        self.cnt[eng] += 1
        ev = (eng, self.cnt[eng])
        self._commit(ev, reads, writes)
        self.ops[eng].append((fn, deps, ev, 1))
        return ev

    def dma(self, eng, fn, reads=(), writes=(), is_output=False):
        deps = self._deps(reads, writes)
        slot = self.ring_next[eng]
        self.ring_next[eng] = (slot + 1) % self.NRING
        key = ("dma", eng, slot)
        prev = self.ring[eng][slot]
        if prev:
            if deps.get(key, 0) < prev:
                deps[key] = prev
        self.ring[eng][slot] = prev + 16
        ev = (key, prev + 16)
        self._commit(ev, reads, writes)
        self.ops[eng].append((fn, deps, ev, 16))
        if is_output:
            self.out_events.append(ev)
        return ev

    def emit(self, nc, stack):
        sems = {}

        def sem(key):
            if key not in sems:
                nm = key if isinstance(key, str) else "d_%s_%d" % (key[1], key[2])
                sems[key] = stack.enter_context(nc.semaphore("s_" + nm))
            return sems[key]
        for e in self.COMPUTE:
            sem(e)
        for e in self.ring:
            for s in range(self.NRING):
                sem(("dma", e, s))
        final = {}
        for (k, v) in self.out_events:
            final[k] = max(final.get(k, 0), v)
        for e in self.COMPUTE:
            final[e] = self.cnt[e]
        ops = self.ops
        block = stack.enter_context(nc.Block())

        def run(eng_name, eng, extra_final=None):
            waited = {}
            for (fn, deps, ev, inc) in ops[eng_name]:
                for k, v in deps.items():
                    if k == eng_name and eng_name == "pe":
                        continue
                    if waited.get(k, 0) < v:
                        eng.wait_ge(sem(k), v)
                        waited[k] = v
                ins = fn(eng)
                ins.then_inc(sem(ev[0]), inc)
            if extra_final:
                for k, v in extra_final.items():
                    if waited.get(k, 0) < v:
                        eng.wait_ge(sem(k), v)

        @block.tensor
        def _(e):
            run("pe", e)

        @block.scalar
        def _(e):
            run("act", e)

        @block.vector
        def _(e):
            run("dve", e)

        @block.gpsimd
        def _(e):
            run("pool", e)

        @block.sync
        def _(e):
            run("sp", e, extra_final=final)


NB = 128
NT = NB + TOK + 4
TILES = [(0, 512), (512, 512), (1024, NT - 1024)]


class Ctx:
    pass


def alias(new_bufs, old_bufs):
    acc = {}
    for b in old_bufs:
        evs = dict(b.r)
        if b.w is not None:
            evs[b.w[0]] = max(evs.get(b.w[0], 0), b.w[1])
        for k, v in evs.items():
            if acc.get(k, 0) < v:
                acc[k] = v
    for b in new_bufs:
        for k, v in acc.items():
            if b.r.get(k, 0) < v:
                b.r[k] = v


def emit_norm(P, C, gidx, tiles):
    alias([b for l in C.h_b for b in l], [b for l in C.m_b for b in l])
    for ti, (c0, cn) in enumerate(tiles):
        bank = C.next_bank()
        for kc in range(KC):
            sq, sqb = C.sq[kc % 2], C.sq_b[kc % 2]
            sqh = sq[:, :].bitcast(BF16)
            P.op("act", lambda e, kc=kc, sqh=sqh, c0=c0, cn=cn: e.activation(
                out=sqh[:, 0:cn], in_=C.xT[:, kc, c0:c0 + cn], func=AF.Square),
                reads=[C.x_b[kc][ti]], writes=[sqb])
            P.op("pe", lambda e, kc=kc, sqh=sqh, cn=cn, bank=bank: e.matmul(
                C.ps[bank][:, 0:cn], lhsT=C.onesb[:, :], rhs=sqh[:, 0:cn],
                start=(kc == 0), stop=(kc == KC - 1)),
                reads=[sqb, C.const_b], writes=[C.ps_b[bank]])
        P.op("act", lambda e, cn=cn, bank=bank: e.activation(
            out=C.rstd[:, 0:cn], in_=C.ps[bank][:, 0:cn], func=AF.Sqrt, bias=C.epsc[:, 0:1], scale=1.0),
            reads=[C.ps_b[bank], C.const_b], writes=[C.rstd_b])
        P.op("dve", lambda e, cn=cn: e.reciprocal(out=C.rstd[:, 0:cn], in_=C.rstd[:, 0:cn]),
             reads=[C.rstd_b], writes=[C.rstd_b])
        for kc in range(KC):
            P.op("dve", lambda e, kc=kc, c0=c0, cn=cn: e.scalar_tensor_tensor(
                out=C.hT[:, kc, c0:c0 + cn], in0=C.xT[:, kc, c0:c0 + cn],
                scalar=C.gains[:, gidx, kc:kc + 1], in1=C.rstd[:, 0:cn],
                op0=ALU.mult, op1=ALU.mult),
                reads=[C.x_b[kc][ti], C.rstd_b], writes=[C.h_b[kc][ti]])


def emit_ffn(P, C, fidx, gidx, tiles):
    emit_norm(P, C, gidx, tiles)
    alias([b for l in C.a_b for b in l] + C.wgu_b, C.arena_mixer_bufs + C.l1_arena_bufs)
    alias([C.wd_b[1]], C.oR_b + [C.t16_b, C.ost_b])
    alias([C.wd_b[0], C.wd_b[1]], C.sample_bufs)
    wgu_d, wd_d = C.wgu_dram, C.wd_dram

    def ug(g):
        gb = g % 2
        for fl in range(G):
            fc = g * G + fl
            slot = C.wgu_next
            C.wgu_next = (slot + 1) % len(C.wgu)
            P.dma("pool", lambda e, slot=slot, fc=fc: e.dma_start(
                out=C.wgu[slot][:, :], in_=wgu_d[fidx, fc, :, :]), writes=[C.wgu_b[slot]])
            for ti, (c0, cn) in enumerate(tiles):
                bg, bu = C.next_bank(), C.next_bank()
                for which, bank in ((0, bg), (1, bu)):
                    def mm(e, slot=slot, which=which, bank=bank, c0=c0, cn=cn):
                        ins = None
                        for kc in range(KC):
                            off = (kc * 2 + which) * 128
                            ins = e.matmul(C.ps[bank][:, 0:cn], lhsT=C.wgu[slot][:, off:off + 128],
                                           rhs=C.hT[:, kc, c0:c0 + cn],
                                           start=(kc == 0), stop=(kc == KC - 1))
                        return ins
                    P.op("pe", mm, reads=[C.wgu_b[slot]] + [C.h_b[kc][ti] for kc in range(KC)],
                         writes=[C.ps_b[bank]])
                sg = C.sg_next
                C.sg_next = (sg + 1) % len(C.sg)
                P.op("act", lambda e, sg=sg, bg=bg, cn=cn: e.activation(
                    out=C.sg[sg][:, 0:cn], in_=C.ps[bg][:, 0:cn], func=AF.Silu),
                    reads=[C.ps_b[bg]], writes=[C.sg_b[sg]])
                P.op("dve", lambda e, sg=sg, bu=bu, gb=gb, fl=fl, c0=c0, cn=cn: e.tensor_tensor(
                    out=C.aT[gb][:, fl, c0:c0 + cn], in0=C.ps[bu][:, 0:cn], in1=C.sg[sg][:, 0:cn],
                    op=ALU.mult), reads=[C.ps_b[bu], C.sg_b[sg]], writes=[C.a_b[gb][fl]])

    def down(g):
        gb = g % 2
        P.dma("pool", lambda e, gb=gb, g=g: e.dma_start(
            out=C.wd[gb][:, :, :], in_=wd_d[fidx, g, :, :, :]), writes=[C.wd_b[gb]])
        for dc in range(KC):
            for ti, (c0, cn) in enumerate(tiles):
                bank = C.next_bank()

                def mm(e, gb=gb, dc=dc, bank=bank, c0=c0, cn=cn):
                    ins = None
                    for fl in range(G):
                        ins = e.matmul(C.ps[bank][:, 0:cn], lhsT=C.wd[gb][:, fl, dc * 128:(dc + 1) * 128],
                                       rhs=C.aT[gb][:, fl, c0:c0 + cn],
                                       start=(fl == 0), stop=(fl == G - 1))
                    return ins
                P.op("pe", mm, reads=[C.wd_b[gb]] + C.a_b[gb], writes=[C.ps_b[bank]])
                P.op("dve", lambda e, dc=dc, bank=bank, c0=c0, cn=cn: e.scalar_tensor_tensor(
                    out=C.xT[:, dc, c0:c0 + cn], in0=C.ps[bank][:, 0:cn], scalar=0.5,
                    in1=C.xT[:, dc, c0:c0 + cn], op0=ALU.mult, op1=ALU.add),
                    reads=[C.ps_b[bank], C.x_b[dc][ti]], writes=[C.x_b[dc][ti]])

    for g in range(NG):
        ug(g)
        if g >= 1:
            down(g - 1)
    down(NG - 1)


BLOCKS = [(i * 128, 128) for i in range(9)] + [(NB + TOK, 4)]
SB_SCALE = 128 ** -0.5
ACW = 5 * 128 + 8 + 1 + 32


GELU_C = 1.5957691216057308


def emit_gelu_tanh(P, C, out_ap, out_b, ps_ap, ps_b, sqi, cn, part_all=False):
    t = C.sq[sqi][:, 0:cn] if part_all else C.sq[sqi][0:cn, :]
    tb = C.sq_b[sqi]
    P.op("act", lambda e: e.activation(out=t, in_=ps_ap, func=AF.Square), reads=[ps_b], writes=[tb])
    P.op("dve", lambda e: e.tensor_scalar(out=t, in0=t, scalar1=0.044715, scalar2=1.0,
                                          op0=ALU.mult, op1=ALU.add), reads=[tb], writes=[tb])
    P.op("dve", lambda e: e.tensor_tensor(out=t, in0=t, in1=ps_ap, op=ALU.mult), reads=[tb, ps_b], writes=[tb])
    P.op("act", lambda e: e.activation(out=t, in_=t, func=AF.Sigmoid, scale=GELU_C), reads=[tb], writes=[tb])
    P.op("dve", lambda e: e.tensor_tensor(out=out_ap, in0=t, in1=ps_ap, op=ALU.mult),
         reads=[tb, ps_b], writes=[out_b])


OTHER_ROWS = 896


def emit_kv(P, C, blocks, row0, to_outputs, proj_group, load_sec):
    for sec in (2, 3, 4, 5):
        slot = sec % 2
        half = sec % 2
        load_sec(sec, slot)
        for bi, (c0, cn) in enumerate(blocks):
            bank = C.next_bank()
            proj_group(slot, bank, c0, cn)
            st = C.stage_next
            C.stage_next = (st + 1) % len(C.stage)
            if sec in (2, 3):
                sqi = bi % 2
                P.op("act", lambda e, sqi=sqi, bank=bank, cn=cn: e.activation(
                    out=C.sq[sqi][0:cn, :], in_=C.ps[bank][0:cn, :], func=AF.Square),
                    reads=[C.ps_b[bank]], writes=[C.sq_b[sqi]])
                P.op("dve", lambda e, sqi=sqi, cn=cn: e.tensor_reduce(
                    out=C.ssq[0:cn, 0:4], in_=C.sq[sqi][0:cn, :].rearrange("p (h d) -> p h d", d=128),
                    axis=mybir.AxisListType.X, op=ALU.add), reads=[C.sq_b[sqi]], writes=[C.ssq_b])
                P.op("act", lambda e, cn=cn: e.activation(
                    out=C.ssq[0:cn, 0:4], in_=C.ssq[0:cn, 0:4], func=AF.Sqrt, bias=C.epsc[0:cn, 0:1],
                    scale=1.0 / 128), reads=[C.ssq_b, C.const_b], writes=[C.ssq_b])
                P.op("dve", lambda e, cn=cn: e.reciprocal(out=C.ssq[0:cn, 0:4], in_=C.ssq[0:cn, 0:4]),
                     reads=[C.ssq_b], writes=[C.ssq_b])
                for h in range(4):
                    P.op("dve", lambda e, h=h, st=st, bank=bank, cn=cn: e.scalar_tensor_tensor(
                        out=C.stage[st][0:cn, h * 128:(h + 1) * 128], in0=C.ps[bank][0:cn, h * 128:(h + 1) * 128],
                        scalar=C.ssq[0:cn, h:h + 1], in1=C.gk[0:cn, :], op0=ALU.mult, op1=ALU.mult),
                        reads=[C.ps_b[bank], C.ssq_b, C.const_b], writes=[C.stage_b[st]])
                dst = C.k_out
            else:
                P.op("act", lambda e, st=st, bank=bank, cn=cn: e.activation(
                    out=C.stage[st][0:cn, :], in_=C.ps[bank][0:cn, :], func=AF.Copy),
                    reads=[C.ps_b[bank]], writes=[C.stage_b[st]])
                dst = C.v_out
            if to_outputs:
                P.dma("sp", lambda e, st=st, dst=dst, c0=c0, cn=cn, half=half: e.dma_start(
                    out=dst[c0:c0 + cn, half * 512:(half + 1) * 512], in_=C.stage[st][0:cn, :]),
                    reads=[C.stage_b[st]], is_output=True)
            scr, scrb = (C.kscr, C.kscr_b) if sec in (2, 3) else (C.vscr, C.vscr_b)
            P.dma("sp", lambda e, st=st, scr=scr, c0=c0, cn=cn, half=half: e.dma_start(
                out=scr[row0 + c0:row0 + c0 + cn, half * 512:(half + 1) * 512], in_=C.stage[st][0:cn, :]),
                reads=[C.stage_b[st]], writes=[scrb])


def emit_even_proj(P, C, tiles=None, only_kv_blocks=None):
    emit_norm(P, C, 1, tiles or TILES)
    ffn_bufs = [b for l in C.a_b for b in l] + C.wgu_b
    alias(C.mix_b_all, ffn_bufs)

    def proj_group(slot, bank, c0, cn):
        ti = min(c0 // 512, 2)

        def mm(e):
            ins = None
            for kc in range(KC):
                ins = e.matmul(C.ps[bank][0:cn, :], lhsT=C.hT[:, kc, c0:c0 + cn],
                               rhs=C.wd[slot][:, kc // 4, (kc % 4) * 512:(kc % 4) * 512 + 512],
                               start=(kc == 0), stop=(kc == KC - 1))
            return ins
        P.op("pe", mm, reads=[C.wd_b[slot]] + [C.h_b[kc][ti] for kc in range(KC)], writes=[C.ps_b[bank]])

    def load_sec(sec, slot):
        P.dma("pool", lambda e: e.dma_start(out=C.wd[slot][:, :, :], in_=C.win_tm[sec, :, :, :]),
              writes=[C.wd_b[slot]])

    C.proj_group, C.load_sec = proj_group, load_sec
    if not only_kv_blocks:
        emit_kv(P, C, BLOCKS, OTHER_ROWS, True, proj_group, load_sec)
    else:
        emit_kv(P, C, only_kv_blocks, 0, False, proj_group, load_sec)
        return

    load_sec(6, 0)
    load_sec(7, 1)
    for bi, (c0, cn) in enumerate(BLOCKS):
        banks = (C.next_bank(), C.next_bank())
        for half in range(2):
            proj_group(half, banks[half], c0, cn)
            emit_gelu_tanh(P, C, C.gel[0:cn, half * 512:(half + 1) * 512], C.gel_b,
                           C.ps[banks[half]][0:cn, :], C.ps_b[banks[half]], half, cn)
        for half in range(2):
            P.op("act", lambda e, half=half, cn=cn: e.activation(
                out=C.sq[half][0:cn, :], in_=C.gel[0:cn, half * 512:(half + 1) * 512], func=AF.Square),
                reads=[C.gel_b], writes=[C.sq_b[half]])
            P.op("dve", lambda e, half=half, cn=cn: e.tensor_reduce(
                out=C.ssq[0:cn, 4 + half:5 + half], in_=C.sq[half][0:cn, :],
                axis=mybir.AxisListType.X, op=ALU.add), reads=[C.sq_b[half]], writes=[C.ssq_b])
        P.op("dve", lambda e, cn=cn: e.tensor_tensor(
            out=C.ssq[0:cn, 6:7], in0=C.ssq[0:cn, 4:5], in1=C.ssq[0:cn, 5:6], op=ALU.add),
            reads=[C.ssq_b], writes=[C.ssq_b])
        P.op("act", lambda e, cn=cn: e.activation(
            out=C.ssq[0:cn, 6:7], in_=C.ssq[0:cn, 6:7], func=AF.Sqrt, bias=C.epsc[0:cn, 0:1],
            scale=1.0 / 1024), reads=[C.ssq_b, C.const_b], writes=[C.ssq_b])
        P.op("dve", lambda e, cn=cn: e.reciprocal(out=C.ssq[0:cn, 6:7], in_=C.ssq[0:cn, 6:7]),
             reads=[C.ssq_b], writes=[C.ssq_b])
        P.op("dve", lambda e, bi=bi, cn=cn: e.scalar_tensor_tensor(
            out=C.avn[0:cn, bi, :], in0=C.gel[0:cn, :], scalar=C.ssq[0:cn, 6:7], in1=C.gav[0:cn, :],
            op0=ALU.mult, op1=ALU.mult), reads=[C.gel_b, C.ssq_b, C.const_b], writes=[C.avn_b[bi]])
        if cn == 4:
            P.op("dve", lambda e, cn=cn: e.scalar_tensor_tensor(
                out=C.gel[0:cn, :], in0=C.gel[0:cn, :], scalar=C.ssq[0:cn, 6:7], in1=C.gav[0:cn, :],
                op0=ALU.mult, op1=ALU.mult), reads=[C.gel_b, C.ssq_b, C.const_b], writes=[C.gel_b])
            P.dma("sp", lambda e, cn=cn: e.dma_start(out=C.avn_out[0:cn, :], in_=C.gel[0:cn, :]),
                  reads=[C.gel_b], is_output=True)


def emit_even_gate(P, C):
    def proj_group(slot, bank, c0, cn):
        ti = min(c0 // 512, 2)

        def mm(e):
            ins = None
            for kc in range(KC):
                ins = e.matmul(C.ps[bank][0:cn, :], lhsT=C.hT[:, kc, c0:c0 + cn],
                               rhs=C.wd[slot][:, kc // 4, (kc % 4) * 512:(kc % 4) * 512 + 512],
                               start=(kc == 0), stop=(kc == KC - 1))
            return ins
        P.op("pe", mm, reads=[C.wd_b[slot]] + [C.h_b[kc][ti] for kc in range(KC)], writes=[C.ps_b[bank]])

    for sec in (0, 1):
        slot = sec % 2
        P.dma("pool", lambda e, sec=sec, slot=slot: e.dma_start(out=C.wd[slot][:, :, :], in_=C.win_tm[sec, :, :, :]),
              writes=[C.wd_b[slot]])
        for bi, (c0, cn) in enumerate(BLOCKS):
            bank = C.next_bank()
            proj_group(slot, bank, c0, cn)
            st = C.stage_next
            C.stage_next = (st + 1) % len(C.stage)
            sqi = bi % 2
            P.op("act", lambda e, sqi=sqi, bank=bank, cn=cn: e.activation(
                out=C.sq[sqi][0:cn, :], in_=C.ps[bank][0:cn, :], func=AF.Square),
                reads=[C.ps_b[bank]], writes=[C.sq_b[sqi]])
            P.op("dve", lambda e, sqi=sqi, cn=cn: e.tensor_reduce(
                out=C.ssq[0:cn, 0:4], in_=C.sq[sqi][0:cn, :].rearrange("p (h d) -> p h d", d=128),
                axis=mybir.AxisListType.X, op=ALU.add), reads=[C.sq_b[sqi]], writes=[C.ssq_b])
            P.op("act", lambda e, cn=cn: e.activation(
                out=C.ssq[0:cn, 0:4], in_=C.ssq[0:cn, 0:4], func=AF.Sqrt, bias=C.epsc[0:cn, 0:1],
                scale=1.0 / 128), reads=[C.ssq_b, C.const_b], writes=[C.ssq_b])
            P.op("dve", lambda e, cn=cn: e.reciprocal(out=C.ssq[0:cn, 0:4], in_=C.ssq[0:cn, 0:4]),
                 reads=[C.ssq_b], writes=[C.ssq_b])
            P.op("dve", lambda e, cn=cn: e.tensor_single_scalar(
                out=C.ssq[0:cn, 0:4], in_=C.ssq[0:cn, 0:4], scalar=SB_SCALE, op=ALU.mult),
                reads=[C.ssq_b], writes=[C.ssq_b])
            for h in range(4):
                P.op("dve", lambda e, h=h, st=st, bank=bank, cn=cn: e.scalar_tensor_tensor(
                    out=C.stage[st][0:cn, h * 128:(h + 1) * 128], in0=C.ps[bank][0:cn, h * 128:(h + 1) * 128],
                    scalar=C.ssq[0:cn, h:h + 1], in1=C.gq[0:cn, :], op0=ALU.mult, op1=ALU.mult),
                    reads=[C.ps_b[bank], C.ssq_b, C.const_b], writes=[C.stage_b[st]])
            P.dma("sp", lambda e, st=st, c0=c0, cn=cn, sec=sec: e.dma_start(
                out=C.qscr[c0:c0 + cn, sec * 512:(sec + 1) * 512], in_=C.stage[st][0:cn, :]),
                reads=[C.stage_b[st]], writes=[C.qscr_b])

    for hf in range(2):
        P.dma("sp", lambda e, hf=hf: e.dma_start(out=C.sq[hf][:, :], in_=C.wsT_d[:, hf * 512:(hf + 1) * 512]),
              writes=[C.sq_b[hf]])
    for g in range(8):
        P.op("dve", lambda e, g=g: e.tensor_tensor(out=C.wmT[:, g, :], in0=C.sq[g // 4][:, (g % 4) * 128:(g % 4 + 1) * 128],
                                                   in1=C.gconst[:, 1024:1152], op=ALU.mult),
             reads=[C.const_b, C.sq_b[g // 4]], writes=[C.wmT_b])

    alias(C.au_b, C.stage_b + [C.gel_b])
    for gq in range(2):
        slot = gq % 2
        P.dma("pool", lambda e, gq=gq, slot=slot: e.dma_start(out=C.wd[slot][:, :, :], in_=C.wau[gq, :, :, :]),
              writes=[C.wd_b[slot]])
        for ocl in range(4):
            for ti, (c0, cn) in enumerate(TILES):
                bank = C.next_bank()

                def mm(e, slot=slot, ocl=ocl, bank=bank, c0=c0, cn=cn):
                    ins = None
                    for kc in range(KC):
                        off = (kc * 4 + ocl) * 128
                        ins = e.matmul(C.ps[bank][:, 0:cn], lhsT=C.wd[slot][:, off // 2048, off % 2048:off % 2048 + 128],
                                       rhs=C.hT[:, kc, c0:c0 + cn], start=(kc == 0), stop=(kc == KC - 1))
                    return ins
                P.op("pe", mm, reads=[C.wd_b[slot]] + [C.h_b[kc][ti] for kc in range(KC)], writes=[C.ps_b[bank]])
                emit_gelu_tanh(P, C, C.auT[:, gq * 4 + ocl, c0:c0 + cn], C.au_b[gq * 4 + ocl], C.ps[bank][:, 0:cn],
                               C.ps_b[bank], (ocl + ti) % 2, cn, part_all=True)

    alias([b for l in C.m_b for b in l], [b for l in C.h_b for b in l])
    for gq in range(2):
        for bi, (c0, cn) in enumerate(BLOCKS):
            bank = C.next_bank()

            def mm(e, bi=bi, bank=bank, cn=cn, gq=gq):
                ins = None
                for gl in range(4):
                    g = gq * 4 + gl
                    ins = e.matmul(C.ps[bank][:, gl * 128:gl * 128 + cn], lhsT=C.avn[0:cn, bi, g * 128:(g + 1) * 128],
                                   rhs=C.wmT[0:cn, g, 0:cn], start=True, stop=True)
                return ins
            P.op("pe", mm, reads=[C.avn_b[bi], C.wmT_b], writes=[C.ps_b[bank]])
            sqi = bi % 2
            pv = C.ps[bank][:, :].rearrange("p (g c) -> p g c", g=4)[:, :, 0:cn]
            tv = C.sq[sqi][:, :].rearrange("p (g c) -> p g c", g=4)[:, :, 0:cn]
            bv = C.bs_bc[:, gq * 512:(gq + 1) * 512].rearrange("p (g c) -> p g c", g=4)[:, :, 0:cn]
            P.op("dve", lambda e, pv=pv, tv=tv, bv=bv: e.tensor_tensor(out=tv, in0=pv, in1=bv, op=ALU.add),
                 reads=[C.ps_b[bank], C.const_b], writes=[C.sq_b[sqi]])
            ti = min(c0 // 512, 2)
            P.op("dve", lambda e, tv=tv, gq=gq, c0=c0, cn=cn: e.tensor_tensor(
                out=C.mixT[:, gq * 4:gq * 4 + 4, c0:c0 + cn], in0=tv, in1=C.auT[:, gq * 4:gq * 4 + 4, c0:c0 + cn],
                op=ALU.mult), reads=[C.sq_b[sqi]] + C.au_b[gq * 4:gq * 4 + 4],
                writes=[C.m_b[gq * 4 + gl][ti] for gl in range(4)])


NKEYB = 16
OTHER = 896


def _rbank(C):
    b = C.next_bank()
    while b in C.reserved:
        b = C.next_bank()
    return b


def gen_attn_prompt(P, C, S, heads):
    QT = [(0, 7), (384, 10), (768, 13)]
    for h in heads:
        hs = slice(h * 128, (h + 1) * 128)
        P.dma("pool", lambda e, hs=hs: e.dma_start(
            out=C.kh[:, :, :], in_=C.kscr[0:2048, hs].rearrange("(b p) d -> p b d", p=128)),
            reads=[C.kscr_b], writes=[C.kh_b])
        P.dma("pool", lambda e, hs=hs: e.dma_start(
            out=S.vh[:, :, :], in_=C.vscr[0:2048, hs].rearrange("(b p) d -> p b d", p=128)),
            reads=[C.vscr_b], writes=[S.vh_b])
        P.dma("pool", lambda e, hs=hs: e.dma_start(
            out=C.qh[:, :, :], in_=C.qscr[0:1152, hs].rearrange("(b p) d -> p b d", p=128)),
            reads=[C.qscr_b], writes=[C.qh_b])
        for (src, srcb, dst, dstb, nblk) in ((C.kh, C.kh_b, S.khT, S.khT_b, 16), (C.qh, C.qh_b, S.qhT, S.qhT_b, 9)):
            for b0 in range(0, nblk, 4):
                nb = min(4, nblk - b0)
                bank = _rbank(C)
                pbf = C.ps[bank][:, :].bitcast(BF16)

                def tr(e, src=src, b0=b0, nb=nb, pbf=pbf):
                    ins = None
                    for i in range(nb):
                        ins = e.transpose(pbf[:, i * 128:(i + 1) * 128], src[:, b0 + i, :], C.ident[:, :])
                    return ins
                P.op("pe", tr, reads=[srcb, C.const_b], writes=[C.ps_b[bank]])
                P.op("dve", lambda e, dst=dst, b0=b0, nb=nb, pbf=pbf: e.tensor_copy(
                    out=dst[:, b0 * 128:(b0 + nb) * 128], in_=pbf[:, 0:nb * 128]),
                    reads=[C.ps_b[bank]], writes=[dstb])
        yield
        bias = C.sbias[:, h:h + 1]
        for (q0, Qa) in QT:
            P.op("dve", lambda e: e.memset(S.sp32[:, :], 0.0), writes=[S.sp32_b])
            obank = _rbank(C)
            C.reserved.add(obank)
            P.op("pe", lambda e, obank=obank: e.matmul(C.ps[obank][:, 0:384], lhsT=C.zerosb[:, :], rhs=S.khT[:, 0:384],
                                                       start=True, stop=False),
                 reads=[C.const_b, S.khT_b], writes=[C.ps_b[obank]])
            Stop = Qa + 2
            for ui, Sk in enumerate(range(Stop, -1, -1)):
                j0 = max(Sk - Qa, 0) * 128
                w = 384 - j0
                diag = Sk >= Qa
                kblk = S.khT[:, Sk * 128:(Sk + 1) * 128]
                qcols = S.qhT[:, q0 + j0:q0 + 384]
                i2 = ui % 2
                zb = _rbank(C)
                P.op("pe", lambda e, zb=zb, w=w, kblk=kblk, qcols=qcols: e.matmul(
                    C.ps[zb][:, 0:w], lhsT=kblk, rhs=qcols, start=True, stop=True),
                    reads=[S.khT_b, S.qhT_b], writes=[C.ps_b[zb]])
                P.op("act", lambda e, zb=zb, w=w, bias=bias: e.activation(
                    out=S.ebuf[:, 0:w], in_=C.ps[zb][:, 0:w], func=AF.Exp, bias=bias, scale=1.0),
                    reads=[C.ps_b[zb], C.const_b], writes=[S.e_b])
                P.op("act", lambda e, w=w, i2=i2: e.activation(
                    out=S.spb[i2][:, 0:w], in_=S.ebuf[:, 0:w], func=AF.Ln, bias=C.onec[:, 0:1], scale=1.0),
                    reads=[S.e_b, C.const_b], writes=[S.sp_b[i2]])
                if diag:
                    P.op("dve", lambda e, i2=i2: e.tensor_tensor(
                        out=S.spb[i2][:, 0:128], in0=S.spb[i2][:, 0:128], in1=C.tris[:, :], op=ALU.mult),
                        reads=[S.sp_b[i2], C.const_b], writes=[S.sp_b[i2]])
                yield
                ab = _rbank(C)

                def mm2(e, ab=ab, w=w, kblk=kblk, qcols=qcols, i2=i2, j0=j0, first=(Sk == Stop)):
                    e.matmul(C.ps[ab][:, 0:w], lhsT=kblk, rhs=qcols, start=True, stop=False)
                    ins = e.matmul(C.ps[ab][:, 0:w], lhsT=C.ntri[:, :], rhs=S.spb[i2][:, 0:w], start=False, stop=first)
                    if not first:
                        ins = e.matmul(C.ps[ab][:, 0:w], lhsT=C.nones[:, :], rhs=S.spbf[:, j0:384], start=False, stop=True)
                    return ins
                P.op("pe", mm2, reads=[S.khT_b, S.qhT_b, S.sp_b[i2], S.spbf_b, C.const_b], writes=[C.ps_b[ab]])
                P.op("act", lambda e, ab=ab, w=w, i2=i2, bias=bias: e.activation(
                    out=S.abuf[i2][:, 0:w], in_=C.ps[ab][:, 0:w], func=AF.Exp, bias=bias, scale=1.0),
                    reads=[C.ps_b[ab], C.const_b], writes=[S.a_b2[i2]])
                if diag:
                    P.op("dve", lambda e, i2=i2: e.tensor_tensor(
                        out=S.abuf[i2][:, 0:128], in0=S.abuf[i2][:, 0:128], in1=C.tris[:, :], op=ALU.mult),
                        reads=[S.a_b2[i2], C.const_b], writes=[S.a_b2[i2]])
                yield
                P.op("pe", lambda e, obank=obank, Sk=Sk, j0=j0, w=w, i2=i2, last=(Sk == 0): e.matmul(
                    C.ps[obank][:, j0:384], lhsT=S.vh[:, Sk, :], rhs=S.abuf[i2][:, 0:w], start=False, stop=last),
                    reads=[S.vh_b, S.a_b2[i2]], writes=[C.ps_b[obank]])
                if Sk > 0:
                    P.op("dve", lambda e, j0=j0, w=w, i2=i2: e.tensor_tensor(
                        out=S.sp32[:, j0:384], in0=S.sp32[:, j0:384], in1=S.spb[i2][:, 0:w], op=ALU.add),
                        reads=[S.sp_b[i2], S.sp32_b], writes=[S.sp32_b])
                    P.op("dve", lambda e: e.tensor_copy(out=S.spbf[:, :], in_=S.sp32[:, :]),
                         reads=[S.sp32_b], writes=[S.spbf_b])
                yield
            for t3 in range(3):
                c0 = q0 + t3 * 128
                ti = min(c0 // 512, 2)
                P.op("act", lambda e, obank=obank, t3=t3, c0=c0, h=h: e.activation(
                    out=C.mixT[:, 8 + h, c0:c0 + 128], in_=C.ps[obank][:, t3 * 128:(t3 + 1) * 128], func=AF.Copy),
                    reads=[C.ps_b[obank]], writes=[C.m_b[8 + h][ti]])
            C.reserved.discard(obank)


def gen_attn_sample(P, C):
    NP = C.n_pages if _DEV["max_pages"] is None else _DEV["max_pages"]
    P.dma("sp", lambda e: e.dma_start(out=C.pt_i[:, :], in_=C.pt_d[:, :]), writes=[C.idx_b])
    P.op("dve", lambda e: e.tensor_copy(out=C.pt_f[:, :], in_=C.pt_i[:, :]), reads=[C.idx_b], writes=[C.idx_b])
    P.op("dve", lambda e: e.tensor_scalar(out=C.pt_f[:, :], in0=C.pt_f[:, :], scalar1=128.0, scalar2=C.aconst32[:, 648:649],
                                          op0=ALU.mult, op1=ALU.add), reads=[C.idx_b, C.const_b], writes=[C.idx_b])
    P.op("dve", lambda e: e.tensor_copy(out=C.pt_i[:, :], in_=C.pt_f[:, :]), reads=[C.idx_b], writes=[C.idx_b])
    P.dma("pool", lambda e: e.dma_start(out=C.sq_new[0:4, :], in_=C.qscr[NB + TOK:NT, :]), reads=[C.qscr_b], writes=[C.s_small_b])
    P.dma("pool", lambda e: e.dma_start(out=C.sk_new[0:4, :], in_=C.kscr[OTHER_ROWS + NB + TOK:OTHER_ROWS + NT, :]),
          reads=[C.kscr_b], writes=[C.s_small_b])
    P.dma("pool", lambda e: e.dma_start(out=C.sv_new[0:4, :], in_=C.vscr[OTHER_ROWS + NB + TOK:OTHER_ROWS + NT, :]),
          reads=[C.vscr_b], writes=[C.s_small_b])
    for (src, dst) in ((C.sq_new, C.sqT), (C.sk_new, C.skT)):
        bank = _rbank(C)
        pbf = C.ps[bank][:, :].bitcast(BF16)

        def tr(e, src=src, pbf=pbf):
            ins = None
            for h in range(8):
                ins = e.transpose(pbf[:, h * 4:(h + 1) * 4], src[0:4, h * 128:(h + 1) * 128], C.ident[0:4, 0:4])
            return ins
        P.op("pe", tr, reads=[C.s_small_b, C.const_b], writes=[C.ps_b[bank]])
        P.op("dve", lambda e, dst=dst, pbf=pbf: e.tensor_copy(out=dst[:, :], in_=pbf[:, 0:32]),
             reads=[C.ps_b[bank]], writes=[C.s_small_b])
    for h in range(8):
        P.op("dve", lambda e, h=h: e.tensor_copy(out=C.mask4[0:4, h * 4:(h + 1) * 4], in_=C.tris[0:4, 0:4]),
             reads=[C.const_b], writes=[C.s_small_b])
    P.op("dve", lambda e: e.memset(C.ssp32[:, :], 0.0), writes=[C.ssp32_b])
    P.op("dve", lambda e: e.memset(C.sspbf[:, :], 0.0), writes=[C.sspbf_b])
    obank = _rbank(C)
    C.reserved.add(obank)
    P.op("pe", lambda e: e.matmul(C.ps[obank][:, 0:32], lhsT=C.zerosb[:, :], rhs=C.sqT[:, :], start=True, stop=False),
         reads=[C.const_b, C.s_small_b], writes=[C.ps_b[obank]])

    def unit(ui, np_, kT_of, kT_b, v_of, v_b, diag, first, last):
        i2 = ui % 2
        zb = _rbank(C)

        def mmz(e):
            ins = None
            for h in range(8):
                ins = e.matmul(C.ps[zb][0:np_, h * 4:(h + 1) * 4], lhsT=kT_of(h), rhs=C.sqT[:, h * 4:(h + 1) * 4],
                               start=True, stop=True)
            return ins
        P.op("pe", mmz, reads=[kT_b, C.s_small_b], writes=[C.ps_b[zb]])
        P.op("dve", lambda e: e.tensor_tensor(out=C.sz[i2][0:np_, :], in0=C.ps[zb][0:np_, 0:32], in1=C.aconst32[0:np_, 649:681],
                                              op=ALU.add), reads=[C.ps_b[zb], C.const_b], writes=[C.sz_b[i2]])
        P.op("act", lambda e: e.activation(out=C.se[0:np_, :], in_=C.sz[i2][0:np_, :], func=AF.Exp),
             reads=[C.sz_b[i2]], writes=[C.se_b])
        P.op("act", lambda e: e.activation(out=C.ssp[i2][0:np_, :], in_=C.se[0:np_, :], func=AF.Ln, bias=C.onec[0:np_, 0:1], scale=1.0),
             reads=[C.se_b, C.const_b], writes=[C.ssp_b[i2]])
        if diag:
            P.op("dve", lambda e: e.tensor_tensor(out=C.ssp[i2][0:np_, :], in0=C.ssp[i2][0:np_, :], in1=C.mask4[0:np_, :], op=ALU.mult),
                 reads=[C.ssp_b[i2], C.s_small_b], writes=[C.ssp_b[i2]])
        yield
        cb = _rbank(C)

        def mmc(e):
            ins = e.matmul(C.ps[cb][0:np_, 0:32], lhsT=C.ntri[0:np_, 0:np_], rhs=C.ssp[i2][0:np_, :], start=True, stop=first)
            if not first:
                ins = e.matmul(C.ps[cb][0:np_, 0:32], lhsT=C.nones[:, 0:np_], rhs=C.sspbf[:, :], start=False, stop=True)
            return ins
        P.op("pe", mmc, reads=[C.ssp_b[i2], C.sspbf_b, C.const_b], writes=[C.ps_b[cb]])
        P.op("dve", lambda e: e.tensor_tensor(out=C.sz[i2][0:np_, :], in0=C.sz[i2][0:np_, :], in1=C.ps[cb][0:np_, 0:32], op=ALU.add),
             reads=[C.ps_b[cb], C.sz_b[i2]], writes=[C.sz_b[i2]])
        P.op("act", lambda e: e.activation(out=C.sab[i2][0:np_, :], in_=C.sz[i2][0:np_, :], func=AF.Exp),
             reads=[C.sz_b[i2]], writes=[C.sab_b[i2]])
        if diag:
            P.op("dve", lambda e: e.tensor_tensor(out=C.sab[i2][0:np_, :], in0=C.sab[i2][0:np_, :], in1=C.mask4[0:np_, :], op=ALU.mult),
                 reads=[C.sab_b[i2], C.s_small_b], writes=[C.sab_b[i2]])

        yield

        def mmo(e):
            ins = None
            for h in range(8):
                ins = e.matmul(C.ps[obank][:, h * 4:(h + 1) * 4], lhsT=v_of(h), rhs=C.sab[i2][0:np_, h * 4:(h + 1) * 4],
                               start=False, stop=(last and h == 7))
            return ins
        P.op("pe", mmo, reads=[v_b, C.sab_b[i2]], writes=[C.ps_b[obank]])
        if not last:
            P.op("dve", lambda e: e.tensor_tensor(out=C.ssp32[0:np_, :], in0=C.ssp32[0:np_, :], in1=C.ssp[i2][0:np_, :], op=ALU.add),
                 reads=[C.ssp_b[i2], C.ssp32_b], writes=[C.ssp32_b])
            P.op("dve", lambda e: e.tensor_copy(out=C.sspbf[:, :], in_=C.ssp32[:, :]), reads=[C.ssp32_b], writes=[C.sspbf_b])

    yield from unit(0, 4, lambda h: C.skT[:, h * 4:(h + 1) * 4], C.s_small_b, lambda h: C.sv_new[0:4, h * 128:(h + 1) * 128], C.s_small_b,
                    True, True, NP == 0)
    yield
    NST = len(C.kpg)
    for ui, j in enumerate(range(NP - 1, -1, -1)):
        pb = ui % NST
        tb = ui % 2
        P.dma("pool", lambda e, pb=pb, j=j: e.indirect_dma_start(
            out=C.kpg[pb][:, :], out_offset=None, in_=C.ck_rows[:, :],
            in_offset=bass.IndirectOffsetOnAxis(ap=C.pt_i[:, j:j + 1], axis=0)), reads=[C.idx_b], writes=[C.kpg_b[pb]])
        P.dma("pool", lambda e, pb=pb, j=j: e.indirect_dma_start(
            out=C.vpg[pb][:, :], out_offset=None, in_=C.cv_rows[:, :],
            in_offset=bass.IndirectOffsetOnAxis(ap=C.pt_i[:, j:j + 1], axis=0)), reads=[C.idx_b], writes=[C.vpg_b[pb]])
        for hq in range(2):
            bank = _rbank(C)
            pbf = C.ps[bank][:, :].bitcast(BF16)

            def tr(e, pb=pb, hq=hq, pbf=pbf):
                ins = None
                for i in range(4):
                    h = hq * 4 + i
                    ins = e.transpose(pbf[:, i * 128:(i + 1) * 128], C.kpg[pb][:, h * 128:(h + 1) * 128], C.ident[:, :])
                return ins
            P.op("pe", tr, reads=[C.kpg_b[pb], C.const_b], writes=[C.ps_b[bank]])
            P.op("dve", lambda e, tb=tb, hq=hq, pbf=pbf: e.tensor_copy(out=C.ktp[tb][:, hq * 512:(hq + 1) * 512], in_=pbf[:, 0:512]),
                 reads=[C.ps_b[bank]], writes=[C.ktp_b[tb]])
        yield
        yield from unit(ui + 1, 128, lambda h, tb=tb: C.ktp[tb][:, h * 128:(h + 1) * 128], C.ktp_b[tb],
                        lambda h, pb=pb: C.vpg[pb][:, h * 128:(h + 1) * 128], C.vpg_b[pb], False, False, j == 0)
        yield
    P.op("act", lambda e: e.activation(
        out=C.mixT[:, 8:16, NB + TOK:NT], in_=C.ps[obank][:, 0:32].rearrange("p (h q) -> p h q", h=8), func=AF.Copy),
        reads=[C.ps_b[obank]], writes=[C.m_b[8 + h][2] for h in range(8)])
    C.reserved.discard(obank)


def emit_attention(P, C):
    prompt_b = [C.kh_b, C.qh_b] + [b for S in C.pstream for b in S.bufs]
    alias(prompt_b, C.avn_b + C.au_b + [C.wmT_b] + C.stage_b + [C.gel_b])
    alias(C.sample_bufs, [C.wd_b[0], C.wd_b[1]])
    gens = [gen_attn_prompt(P, C, C.pstream[0], [0, 2, 4, 6]), gen_attn_prompt(P, C, C.pstream[1], [1, 3, 5, 7])]
    if not _DEV["no_sample"]:
        gens.append(gen_attn_sample(P, C))
    alive = list(gens)
    while alive:
        for g in list(alive):
            try:
                next(g)
            except StopIteration:
                alive.remove(g)
    alias([C.wd_b[0], C.wd_b[1]], C.sample_bufs)


PW = NB + TOK


def emit_odd_mixer(P, C):
    emit_norm(P, C, 4, TILES)
    R = C.oR
    Rb = C.oR_b
    l1 = [C.pooled_b[i] for i in range(8)] + [C.m2_b[i] for i in range(8)] + [C.cwg_b, C.hist_b] + Rb
    alias(l1 + [C.t16_b, C.ost_b], C.arena_mixer_bufs + [b for l in C.a_b for b in l] + C.wgu_b + [C.wd_b[1]])
    P.dma("pool", lambda e: e.dma_start(out=C.cwg[:, :], in_=C.cwg_d[:, :]), writes=[C.cwg_b])
    P.dma("sp", lambda e: e.dma_start(out=C.xs_hist[:, :, 0:15], in_=C.pool_hist_d[:, :, :]), writes=[C.hist_b])
    P.dma("sp", lambda e: e.dma_start(out=C.gs[:, :, 0:2], in_=C.conv_hist_d[:, :, :]), writes=[C.hist_b])

    def project(ch, kind, ti, c0, cn):
        bank = C.next_bank()

        def mm(e):
            ins = None
            for kc in range(KC):
                off = (kc * 4 + kind) * 128
                ins = e.matmul(C.ps[bank][:, 0:cn], lhsT=C.wd[0][:, off // 2048, off % 2048:off % 2048 + 128],
                               rhs=C.hT[:, kc, c0:c0 + cn], start=(kc == 0), stop=(kc == KC - 1))
            return ins
        P.op("pe", mm, reads=[C.wd_b[0]] + [C.h_b[kc][ti] for kc in range(KC)], writes=[C.ps_b[bank]])
        return bank

    for ch in range(8):
        g = ch // 2
        w = 2 ** (g + 1)
        P.dma("pool", lambda e, ch=ch: e.dma_start(out=C.wd[0][:, :, :], in_=C.win_odd[ch, :, :, :]), writes=[C.wd_b[0]])
        for ti, (c0, cn) in enumerate(TILES):
            bank = project(ch, 0, ti, c0, cn)
            P.op("act", lambda e, bank=bank, c0=c0, cn=cn: e.activation(out=R[0][:, c0:c0 + cn], in_=C.ps[bank][:, 0:cn], func=AF.Copy),
                 reads=[C.ps_b[bank]], writes=[Rb[0]])
        X = R[0]
        P.op("dve", lambda e, ch=ch: e.tensor_copy(out=C.xs_hist[:, ch, 15:19], in_=X[:, PW:NT]), reads=[Rb[0], C.hist_b], writes=[C.hist_b])
        P.op("dve", lambda e, ch=ch: e.tensor_copy(out=C.ost[:, ch, 0:15], in_=X[:, PW - 15:PW]), reads=[Rb[0]], writes=[C.ost_b])
        P.op("dve", lambda e, ch=ch: e.tensor_copy(out=C.ost[:, ch, 15:30], in_=C.xs_hist[:, ch, 4:19]), reads=[C.hist_b], writes=[C.ost_b])
        src, srcb = X, Rb[0]
        for lv in range(g + 1):
            sh = 2 ** lv
            dst, dstb = (R[1], Rb[1]) if lv % 2 == 0 else (R[2], Rb[2])
            P.op("dve", lambda e, src=src, dst=dst, sh=sh: e.tensor_tensor(
                out=dst[:, sh:PW], in0=src[:, sh:PW], in1=src[:, 0:PW - sh], op=ALU.add), reads=[srcb], writes=[dstb])
            src, srcb = dst, dstb
        P.op("dve", lambda e, src=src, ch=ch, w=w: e.scalar_tensor_tensor(
            out=C.pooled[:, ch, 16:PW], in0=src[:, 16:PW], scalar=1.0 / w, in1=X[:, 16:PW], op0=ALU.mult, op1=ALU.subtract),
            reads=[srcb, Rb[0]], writes=[C.pooled_b[ch]])
        P.op("dve", lambda e, src=src, g=g: e.tensor_tensor(out=C.t16[:, 0:16], in0=src[:, NB:NB + 16], in1=C.oconst[:, 32 + g * 16:32 + (g + 1) * 16],
                                                            op=ALU.mult), reads=[srcb, C.const_b], writes=[C.t16_b])
        P.op("dve", lambda e, ch=ch: e.tensor_tensor(out=C.pooled[:, ch, NB:NB + 16], in0=C.t16[:, 0:16], in1=X[:, NB:NB + 16], op=ALU.subtract),
             reads=[C.t16_b, Rb[0]], writes=[C.pooled_b[ch]])
        hs, hsb = C.xs_hist[:, ch, :], C.hist_b
        cur = hs
        for lv in range(g + 1):
            sh = 2 ** lv
            nxt = C.t16[:, 32 + (lv % 2) * 32:32 + (lv % 2) * 32 + 19]
            P.op("dve", lambda e, cur=cur, nxt=nxt, sh=sh: e.tensor_tensor(out=nxt[:, sh:19], in0=cur[:, sh:19], in1=cur[:, 0:19 - sh], op=ALU.add),
                 reads=[C.hist_b, C.t16_b], writes=[C.t16_b])
            cur = nxt
        P.op("dve", lambda e, cur=cur, ch=ch, w=w, hs=hs: e.scalar_tensor_tensor(
            out=C.pooled[:, ch, PW:NT], in0=cur[:, 15:19], scalar=1.0 / w, in1=hs[:, 15:19], op0=ALU.mult, op1=ALU.subtract),
            reads=[C.t16_b, C.hist_b], writes=[C.pooled_b[ch]])
        for ti, (c0, cn) in enumerate(TILES):
            bgc = project(ch, 2, ti, c0, cn)
            bgh = project(ch, 3, ti, c0, cn)
            bgb = project(ch, 1, ti, c0, cn)
            P.op("act", lambda e, b=bgc, c0=c0, cn=cn: e.activation(out=R[1][:, c0:c0 + cn], in_=C.ps[b][:, 0:cn], func=AF.Copy),
                 reads=[C.ps_b[bgc]], writes=[Rb[1]])
            P.op("dve", lambda e, b=bgh, c0=c0, cn=cn: e.tensor_tensor(out=R[2][:, c0:c0 + cn], in0=R[1][:, c0:c0 + cn], in1=C.ps[b][:, 0:cn], op=ALU.mult),
                 reads=[C.ps_b[bgh], Rb[1]], writes=[Rb[2]])
            P.op("act", lambda e, b=bgb, c0=c0, cn=cn: e.activation(out=R[0][:, c0:c0 + cn], in_=C.ps[b][:, 0:cn], func=AF.Copy),
                 reads=[C.ps_b[bgb]], writes=[Rb[0]])
        Gm, GB = R[2], R[0]
        P.op("dve", lambda e, ch=ch: e.tensor_copy(out=C.gs[:, ch, 2:6], in_=Gm[:, PW:NT]), reads=[Rb[2], C.hist_b], writes=[C.hist_b])
        P.op("dve", lambda e, ch=ch: e.tensor_copy(out=C.ost[:, ch, 30:32], in_=Gm[:, PW - 2:PW]), reads=[Rb[2]], writes=[C.ost_b])
        P.op("dve", lambda e, ch=ch: e.tensor_copy(out=C.ost[:, ch, 32:34], in_=C.gs[:, ch, 4:6]), reads=[C.hist_b], writes=[C.ost_b])
        w0, w1, w2 = (C.oconst[:, 8 + ch * 3 + tap:8 + ch * 3 + tap + 1] for tap in range(3))
        T = R[1]
        P.op("dve", lambda e, w2=w2: e.tensor_scalar(out=T[:, 2:PW], in0=Gm[:, 2:PW], scalar1=w2, scalar2=0.0, op0=ALU.mult, op1=ALU.add),
             reads=[Rb[2], C.const_b], writes=[Rb[1]])
        P.op("dve", lambda e, w1=w1: e.scalar_tensor_tensor(out=T[:, 2:PW], in0=Gm[:, 1:PW - 1], scalar=w1, in1=T[:, 2:PW], op0=ALU.mult, op1=ALU.add),
             reads=[Rb[2], Rb[1], C.const_b], writes=[Rb[1]])
        P.op("dve", lambda e, w0=w0: e.scalar_tensor_tensor(out=T[:, 2:PW], in0=Gm[:, 0:PW - 2], scalar=w0, in1=T[:, 2:PW], op0=ALU.mult, op1=ALU.add),
             reads=[Rb[2], Rb[1], C.const_b], writes=[Rb[1]])
        P.op("dve", lambda e, ch=ch: e.tensor_tensor(out=C.mix2[:, ch, 2:PW], in0=T[:, 2:PW], in1=GB[:, 2:PW], op=ALU.mult),
             reads=[Rb[1], Rb[0]], writes=[C.m2_b[ch]])
        P.op("dve", lambda e, ch=ch: e.memset(C.mix2[:, ch, 0:2], 0.0), writes=[C.m2_b[ch]])
        gsr = C.gs[:, ch, :]
        ts = C.t16[:, 96:100]
        P.op("dve", lambda e, gsr=gsr, ts=ts, w2=w2: e.tensor_scalar(out=ts, in0=gsr[:, 2:6], scalar1=w2, scalar2=0.0, op0=ALU.mult, op1=ALU.add),
             reads=[C.hist_b, C.const_b], writes=[C.t16_b])
        P.op("dve", lambda e, gsr=gsr, ts=ts, w1=w1: e.scalar_tensor_tensor(out=ts, in0=gsr[:, 1:5], scalar=w1, in1=ts, op0=ALU.mult, op1=ALU.add),
             reads=[C.hist_b, C.t16_b, C.const_b], writes=[C.t16_b])
        P.op("dve", lambda e, gsr=gsr, ts=ts, w0=w0: e.scalar_tensor_tensor(out=ts, in0=gsr[:, 0:4], scalar=w0, in1=ts, op0=ALU.mult, op1=ALU.add),
             reads=[C.hist_b, C.t16_b, C.const_b], writes=[C.t16_b])
        P.op("dve", lambda e, ts=ts, ch=ch: e.tensor_tensor(out=C.mix2[:, ch, PW:NT], in0=ts, in1=GB[:, PW:NT], op=ALU.mult),
             reads=[C.t16_b, Rb[0]], writes=[C.m2_b[ch]])
    for ch in range(8):
        P.op("dve", lambda e, ch=ch: e.memset(C.pooled[:, ch, 0:16], 0.0), writes=[C.pooled_b[ch]])
    P.dma("sp", lambda e: e.dma_start(out=C.ost_d[:, :, :], in_=C.ost[:, :, :]), reads=[C.ost_b], is_output=True)
    alias([b for l in C.m_b for b in l], [b for l in C.h_b for b in l])
    for g in range(4):
        for oc in range(2):
            for ti, (c0, cn) in enumerate(TILES):
                bank = C.next_bank()

                def mm(e, g=g, oc=oc, bank=bank, c0=c0, cn=cn):
                    ins = None
                    for ic in range(2):
                        off = ((g * 2 + ic) * 2 + oc) * 128
                        ins = e.matmul(C.ps[bank][:, 0:cn], lhsT=C.cwg[:, off:off + 128], rhs=C.pooled[:, g * 2 + ic, c0:c0 + cn],
                                       start=(ic == 0), stop=(ic == 1))
                    return ins
                P.op("pe", mm, reads=[C.cwg_b, C.pooled_b[g * 2], C.pooled_b[g * 2 + 1]], writes=[C.ps_b[bank]])
                P.op("act", lambda e, g=g, oc=oc, bank=bank, c0=c0, cn=cn: e.activation(
                    out=C.mixT[:, g * 2 + oc, c0:c0 + cn], in_=C.ps[bank][:, 0:cn], func=AF.Copy,
                    scale=C.oconst[:, g * 2 + oc:g * 2 + oc + 1]), reads=[C.ps_b[bank], C.const_b], writes=[C.m_b[g * 2 + oc][ti]])


def emit_even_out(P, C, wout=None, hi=None, hi_b=None):
    wout = C.wout if wout is None else wout
    mixc = (lambda fc: C.mixT[:, fc, :]) if hi is None else (lambda fc: C.mixT[:, fc, :] if fc < 8 else hi[:, fc - 8, :])
    mixb = (lambda fc, ti: C.m_b[fc][ti]) if hi is None else (lambda fc, ti: C.m_b[fc][ti] if fc < 8 else hi_b[fc - 8])
    if _DEV["no_attn"] and hi is None:
        for fc in range(8, 16):
            for ti, (c0, cn) in enumerate(TILES):
                P.op("dve", lambda e, fc=fc, c0=c0, cn=cn: e.memset(C.mixT[:, fc, c0:c0 + cn], 0.0),
                     writes=[C.m_b[fc][ti]])
    for dq in range(4):
        slot = dq % 2
        P.dma("pool", lambda e, dq=dq, slot=slot: e.dma_start(out=C.wd[slot][:, :, :], in_=wout[dq, :, :, :]),
              writes=[C.wd_b[slot]])
        for dl in range(4):
            dc = dq * 4 + dl
            for ti, (c0, cn) in enumerate(TILES):
                bank = C.next_bank()

                def mm(e, slot=slot, dl=dl, bank=bank, c0=c0, cn=cn):
                    ins = None
                    for fc in range(KC):
                        off = fc * 512 + dl * 128
                        ins = e.matmul(C.ps[bank][:, 0:cn], lhsT=C.wd[slot][:, off // 2048, off % 2048:off % 2048 + 128],
                                       rhs=mixc(fc)[:, c0:c0 + cn], start=(fc == 0), stop=(fc == KC - 1))
                    return ins
                P.op("pe", mm, reads=[C.wd_b[slot]] + [mixb(fc, ti) for fc in range(KC)], writes=[C.ps_b[bank]])
                P.op("dve", lambda e, dc=dc, bank=bank, c0=c0, cn=cn: e.tensor_tensor(
                    out=C.xT[:, dc, c0:c0 + cn], in0=C.ps[bank][:, 0:cn], in1=C.xT[:, dc, c0:c0 + cn], op=ALU.add),
                    reads=[C.ps_b[bank], C.x_b[dc][ti]], writes=[C.x_b[dc][ti]])


def build_program(n_phys=1280, n_pages=128):
    import contextlib
    nc = bass.Bass("TRN2", target_bir_lowering=False)
    stack = contextlib.ExitStack()
    C = Ctx()
    P = Prog()
    xT_d = nc.dram_tensor("xT", [128, KC, NT], F32, kind="ExternalInput").ap()
    gains_d = nc.dram_tensor("gains", [128, 6, KC], F32, kind="ExternalInput").ap()
    if not _DEV["skip_ffn"]:
        C.wgu_dram = nc.dram_tensor("wgu", [4, FC, 128, KC * 2 * 128], F32, kind="ExternalInput").ap()
        C.wd_dram = nc.dram_tensor("wd", [4, NG, 128, G, D], F32, kind="ExternalInput").ap()
    yT_d = nc.dram_tensor("yT", [128, KC, NT], F32, kind="ExternalOutput").ap()
    C.win_tm = nc.dram_tensor("win_tm", [8, 128, 4, 2048], F32, kind="ExternalInput").ap()
    grep_d = nc.dram_tensor("g_rep", [128, 128 + 128 + 1024], F32, kind="ExternalInput").ap()
    C.wau = nc.dram_tensor("wau", [2, 128, 4, 2048], F32, kind="ExternalInput").ap()
    C.wout = nc.dram_tensor("wout", [4, 128, 4, 2048], F32, kind="ExternalInput").ap()
    gconst_d = nc.dram_tensor("gconst", [128, 8 * 128 + 128], F32, kind="ExternalInput").ap()
    C.wsT_d = nc.dram_tensor("wsT", [128, 8 * 128], F32, kind="ExternalInput").ap()
    C.qscr = nc.dram_tensor("qscr", [NT, 1024], F32).ap()
    C.kscr = nc.dram_tensor("kscr", [896 + NT, 1024], F32).ap()
    C.vscr = nc.dram_tensor("vscr", [896 + NT, 1024], F32).ap()
    aconst_d = nc.dram_tensor("aconst", [128, ACW], F32, kind="ExternalInput").ap()
    xo_d = nc.dram_tensor("xoT", [128, KC, OTHER_ROWS], F32, kind="ExternalInput").ap()
    C.ck_rows = nc.dram_tensor("ck_rows", [n_phys * 128, 1024], F32, kind="ExternalInput").ap()
    C.cv_rows = nc.dram_tensor("cv_rows", [n_phys * 128, 1024], F32, kind="ExternalInput").ap()
    C.pt_d = nc.dram_tensor("pt_rep", [128, n_pages], I32, kind="ExternalInput").ap()
    C.n_pages = n_pages
    C.win_odd = nc.dram_tensor("win_odd", [8, 128, 4, 2048], F32, kind="ExternalInput").ap()
    C.wout_odd = nc.dram_tensor("wout_odd", [4, 128, 4, 2048], F32, kind="ExternalInput").ap()
    C.cwg_d = nc.dram_tensor("cwg", [128, 2048], F32, kind="ExternalInput").ap()
    oconst_d = nc.dram_tensor("oconst", [128, 96], F32, kind="ExternalInput").ap()
    C.pool_hist_d = nc.dram_tensor("pool_hist", [128, 8, 15], F32, kind="ExternalInput").ap()
    C.conv_hist_d = nc.dram_tensor("conv_hist", [128, 8, 2], F32, kind="ExternalInput").ap()
    C.ost_d = nc.dram_tensor("ost", [128, 8, 34], F32, kind="ExternalOutput").ap()
    C.k_out = nc.dram_tensor("k_out", [NT, 1024], F32, kind="ExternalOutput").ap()
    C.v_out = nc.dram_tensor("v_out", [NT, 1024], F32, kind="ExternalOutput").ap()
    C.avn_out = nc.dram_tensor("avn_out", [4, 1024], F32, kind="ExternalOutput").ap()

    def sb(name, shape, dt):
        return stack.enter_context(nc.sbuf_tensor(name, shape, dt))
    C.xT = sb("xT_sb", [128, KC, NT], F32)
    C.hT = sb("hT_sb", [128, KC, NT], BF16)
    C.gains = sb("gains_sb", [128, 6, KC], F32)
    C.ones32 = sb("ones32", [128, 128], F32)
    C.epsc = sb("epsc", [128, 1], F32)
    C.onesb = sb("onesb", [128, 128], BF16)
    C.sq = [sb("sq%d" % i, [128, 512], F32) for i in range(2)]
    C.rstd = sb("rstd", [128, 512], F32)
    C.sg = [sb("sg%d" % i, [128, 512], BF16) for i in range(3)]
    A_AT = G * NT
    ARENA = 2 * A_AT + 3 * 4096
    C.arena = sb("arena", [128, ARENA], BF16)
    C.aT = [C.arena[:, i * A_AT:(i + 1) * A_AT].rearrange("p (g t) -> p g t", g=G) for i in range(2)]
    C.wgu = [C.arena[:, 2 * A_AT + i * 4096:2 * A_AT + (i + 1) * 4096] for i in range(3)]
    a32 = C.arena[:, :].bitcast(F32)
    C.avn = C.arena[:, 0:10240].rearrange("p (b f) -> p b f", b=10)
    C.wmT = C.arena[:, 10240:11264].rearrange("p (g c) -> p g c", g=8)
    C.stage = [a32[:, 5632 + i * 512:5632 + (i + 1) * 512] for i in range(3)]
    C.gel = a32[:, 7168:8192]
    C.auT = C.arena[:, 11264:11264 + 8 * NT].rearrange("p (g t) -> p g t", g=8)
    C.g_rep = sb("g_rep_sb", [128, 128 + 128 + 1024], F32)
    C.gq, C.gk, C.gav = C.g_rep[:, 0:128], C.g_rep[:, 128:256], C.g_rep[:, 256:1280]
    C.ssq = sb("ssq", [128, 8], F32)
    C.gconst = sb("gconst_sb", [128, 8 * 128 + 128], F32)
    C.bs_bc = C.gconst[:, 0:1024]
    C.mixT = C.hT
    C.au_b = [Buf("au%d" % i) for i in range(8)]
    C.wmT_b = Buf("wmT")
    C.qscr_b = Buf("qscr")
    C.kscr_b, C.vscr_b = Buf("kscr"), Buf("vscr")
    C.aconst32 = sb("aconst32", [128, ACW], F32)
    C.aconst = sb("aconst_bf", [128, 5 * 128], BF16)
    C.ident, C.tris, C.ntri, C.nones, C.zerosb = (C.aconst[:, i * 128:(i + 1) * 128] for i in range(5))
    C.sbias = C.aconst32[:, 640:648]
    C.onec = sb("onec", [128, 1], F32)
    C.m_b = [[Buf("m%d_%d" % (fc, t)) for t in range(len(TILES))] for fc in range(KC)]
    C.stage_b = [Buf("stage%d" % i) for i in range(3)]
    C.gel_b = Buf("gel")
    C.avn_b = [Buf("avn%d" % i) for i in range(10)]
    C.ssq_b = Buf("ssq")
    C.mix_b_all = C.stage_b + [C.gel_b] + C.avn_b
    C.mix_b_all2 = None
    C.stage_next = 0
    C.wd = [sb("wd%d" % i, [128, G, D], BF16) for i in range(2)]
    C.ps = [stack.enter_context(nc.psum_tensor("ps%d" % i, [128, 512], F32)) for i in range(8)]
    C.x_b = [[Buf("x%d_%d" % (kc, t)) for t in range(len(TILES))] for kc in range(KC)]
    C.h_b = [[Buf("h%d_%d" % (kc, t)) for t in range(len(TILES))] for kc in range(KC)]
    C.sq_b = [Buf("sq0"), Buf("sq1")]
    C.rstd_b = Buf("rstd")
    C.sg_b = [Buf("sg%d" % i) for i in range(3)]
    C.a_b = [[Buf("a%d_%d" % (i, f)) for f in range(G)] for i in range(2)]
    C.wgu_b = [Buf("wgu%d" % i) for i in range(3)]
    C.wd_b = [Buf("wd%d" % i) for i in range(2)]
    C.ps_b = [Buf("ps%d" % i) for i in range(8)]
    C.const_b = Buf("const")
    C.wgu_next = 0
    C.sg_next = 0
    C._bank = 0

    def next_bank():
        b = C._bank
        C._bank = (b + 1) % 8
        return b
    C.next_bank = next_bank

    C.pstream = []
    for si in range(2):
        S = Ctx()
        o = si * 8704
        S.vh = C.arena[:, o:o + 2048].rearrange("p (b d) -> p b d", b=16)
        S.khT = C.arena[:, o + 2048:o + 4096]
        S.qhT = C.arena[:, o + 4096:o + 5248]
        S.ebuf = a32[:, (o + 5248) // 2:(o + 5248) // 2 + 384]
        S.spb = [C.arena[:, o + 6016 + i * 384:o + 6016 + (i + 1) * 384] for i in range(2)]
        S.abuf = [C.arena[:, o + 6784 + i * 384:o + 6784 + (i + 1) * 384] for i in range(2)]
        S.sp32 = a32[:, (o + 7552) // 2:(o + 7552) // 2 + 384]
        S.spbf = C.arena[:, o + 8320:o + 8704]
        S.vh_b, S.khT_b, S.qhT_b, S.e_b, S.sp32_b, S.spbf_b = (Buf("%s%d" % (n, si)) for n in ("vh", "khT", "qhT", "e", "sp32", "spbf"))
        S.sp_b = [Buf("sp%d_%d" % (si, i)) for i in range(2)]
        S.a_b2 = [Buf("ab%d_%d" % (si, i)) for i in range(2)]
        S.bufs = [S.vh_b, S.khT_b, S.qhT_b, S.e_b, S.sp32_b, S.spbf_b] + S.sp_b + S.a_b2
        C.pstream.append(S)
    C.kh = C.arena[:, 17408:19456].rearrange("p (b d) -> p b d", b=16)
    C.qh = C.arena[:, 19456:20608].rearrange("p (b d) -> p b d", b=9)
    C.kh_b, C.qh_b = Buf("kh"), Buf("qh")
    C.reserved = set()
    w0 = C.wd[0][:, :, :].rearrange("p a b -> p (a b)")
    w1 = C.wd[1][:, :, :].rearrange("p a b -> p (a b)")
    w1_32, w1_i32 = w1.bitcast(F32), w1.bitcast(I32)
    C.kpg = [w0[:, i * 1024:(i + 1) * 1024] for i in range(4)]
    C.vpg = [w0[:, 4096 + i * 1024:4096 + (i + 1) * 1024] for i in range(4)]
    C.ktp = [w1[:, i * 1024:(i + 1) * 1024] for i in range(2)]
    C.sq_new, C.sk_new, C.sv_new = (w1[:, 2048 + i * 1024:2048 + (i + 1) * 1024] for i in range(3))
    C.sqT, C.skT, C.mask4 = (w1[:, 5120 + i * 32:5120 + (i + 1) * 32] for i in range(3))
    C.sz = [w1_32[:, 2608 + i * 32:2608 + (i + 1) * 32] for i in range(2)]
    C.se = w1_32[:, 2672:2704]
    C.ssp = [w1[:, 5408 + i * 32:5408 + (i + 1) * 32] for i in range(2)]
    C.sab = [w1[:, 5472 + i * 32:5472 + (i + 1) * 32] for i in range(2)]
    C.ssp32 = w1_32[:, 2768:2800]
    C.sspbf = w1[:, 5600:5632]
    C.pt_i = w1_i32[:, 2816:2816 + n_pages]
    C.pt_f = w1_32[:, 2816 + n_pages:2816 + 2 * n_pages]
    C.kpg_b, C.vpg_b = ([Buf("%s%d" % (n, i)) for i in range(4)] for n in ("kpg", "vpg"))
    C.ktp_b = [Buf("ktp%d" % i) for i in range(2)]
    C.s_small_b, C.se_b, C.ssp32_b, C.sspbf_b, C.idx_b = (Buf(n) for n in ("ssmall", "se", "ssp32", "sspbf", "idx"))
    C.sz_b, C.ssp_b, C.sab_b = ([Buf("%s%d" % (n, i)) for i in range(2)] for n in ("sz", "ssp", "sab"))
    C.sample_bufs = (C.kpg_b + C.vpg_b + C.ktp_b + [C.s_small_b, C.se_b, C.ssp32_b, C.sspbf_b, C.idx_b]
                     + C.sz_b + C.ssp_b + C.sab_b)
    C.pooled = C.arena[:, 0:8 * NT].rearrange("p (g t) -> p g t", g=8)
    C.mix2 = C.arena[:, 8 * NT:16 * NT].rearrange("p (g t) -> p g t", g=8)
    C.cwg = C.arena[:, 16 * NT:16 * NT + 2048]
    o32 = 8 * NT + 1024
    C.xs_hist = a32[:, o32:o32 + 152].rearrange("p (g t) -> p g t", g=8)
    C.gs = a32[:, o32 + 152:o32 + 200].rearrange("p (g t) -> p g t", g=8)
    wd1_32 = C.wd[1][:, :, :].rearrange("p a b -> p (a b)").bitcast(F32)
    C.oR = [wd1_32[:, i * NT:(i + 1) * NT] for i in range(3)]
    C.t16 = wd1_32[:, 3 * NT:3 * NT + 128]
    C.ost = wd1_32[:, 3 * NT + 128:3 * NT + 128 + 272].rearrange("p (g t) -> p g t", g=8)
    C.oconst = sb("oconst_sb", [128, 96], F32)
    C.pooled_b = [Buf("pooled%d" % i) for i in range(8)]
    C.m2_b = [Buf("m2_%d" % i) for i in range(8)]
    C.cwg_b, C.hist_b, C.t16_b, C.ost_b = (Buf(n) for n in ("cwg", "hist", "t16", "ost"))
    C.oR_b = [Buf("oR%d" % i) for i in range(3)]
    C.l1_arena_bufs = C.pooled_b + C.m2_b + [C.cwg_b, C.hist_b]
    C.arena_mixer_bufs = (C.stage_b + [C.gel_b] + C.avn_b + C.au_b + [C.wmT_b, C.kh_b, C.qh_b]
                          + [b for S in C.pstream for b in S.bufs])
    all_x = [C.x_b[kc][t] for kc in range(KC) for t in range(len(TILES))]
    P.dma("sp", lambda e: e.dma_start(out=C.gains[:, :, :], in_=gains_d[:, :, :]), writes=[C.const_b])
    P.dma("sp", lambda e: e.dma_start(out=C.g_rep[:, :], in_=grep_d[:, :]), writes=[C.const_b])
    P.dma("sp", lambda e: e.dma_start(out=C.gconst[:, :], in_=gconst_d[:, :]), writes=[C.const_b])
    P.dma("sp", lambda e: e.dma_start(out=C.aconst32[:, :], in_=aconst_d[:, :]), writes=[C.const_b])
    P.op("dve", lambda e: e.tensor_copy(out=C.aconst[:, :], in_=C.aconst32[:, 0:640]), reads=[C.const_b], writes=[C.const_b])
    P.op("dve", lambda e: e.memset(C.onec[:, :], 1.0), writes=[C.const_b])
    P.dma("sp", lambda e: e.dma_start(out=C.oconst[:, :], in_=oconst_d[:, :]), writes=[C.const_b])
    P.op("dve", lambda e: e.memset(C.ones32[:, :], 1.0 / D), writes=[C.const_b])
    P.op("dve", lambda e: e.memset(C.epsc[:, :], EPS), writes=[C.const_b])
    P.op("dve", lambda e: e.memset(C.onesb[:, :], 1.0 / D), writes=[C.const_b])
    TILES0 = [(0, 512), (512, OTHER_ROWS - 512)]
    for ti, (c0, cn) in enumerate(TILES0):
        for kh in range(2):
            P.dma("sp", lambda e, c0=c0, cn=cn, kh=kh: e.dma_start(
                out=C.xT[:, kh * 8:(kh + 1) * 8, c0:c0 + cn], in_=xo_d[:, kh * 8:(kh + 1) * 8, c0:c0 + cn]),
                writes=[C.x_b[kc][ti] for kc in range(kh * 8, (kh + 1) * 8)])
    if not _DEV["skip_ffn"]:
        emit_ffn(P, C, 0, 0, TILES0)
    if not _DEV["only_l1"]:
        emit_even_proj(P, C, tiles=TILES0, only_kv_blocks=[(i * 128, 128) for i in range(OTHER_ROWS // 128)])
    for ti, (c0, cn) in enumerate(TILES):
        for kh in range(2):
            P.dma("sp", lambda e, c0=c0, cn=cn, kh=kh: e.dma_start(
                out=C.xT[:, kh * 8:(kh + 1) * 8, c0:c0 + cn], in_=xT_d[:, kh * 8:(kh + 1) * 8, c0:c0 + cn]),
                writes=[C.x_b[kc][ti] for kc in range(kh * 8, (kh + 1) * 8)])
    for b in all_x:
        pass

    if not _DEV["skip_ffn"]:
        emit_ffn(P, C, 0, 0, TILES)
    if not _DEV["only_l1"]:
        emit_even_proj(P, C)
        emit_even_gate(P, C)
    if not _DEV["no_attn"] and not _DEV["only_l1"]:
        emit_attention(P, C)
    if not _DEV["only_l1"]:
        emit_even_out(P, C)
    if not _DEV["skip_ffn"]:
        emit_ffn(P, C, 1, 2, TILES)
        emit_ffn(P, C, 2, 3, TILES)
    if not _DEV["only_l0"]:
        emit_odd_mixer(P, C)
        emit_even_out(P, C, wout=C.wout_odd, hi=C.mix2, hi_b=C.m2_b)
    if not _DEV["skip_ffn"]:
        emit_ffn(P, C, 3, 5, TILES)

    for ti, (c0, cn) in enumerate(TILES):
        for kh in range(2):
            P.dma("sp", lambda e, c0=c0, cn=cn, kh=kh: e.dma_start(
                out=yT_d[:, kh * 8:(kh + 1) * 8, c0:c0 + cn], in_=C.xT[:, kh * 8:(kh + 1) * 8, c0:c0 + cn]),
                reads=[C.x_b[kc][ti] for kc in range(kh * 8, (kh + 1) * 8)], is_output=True)
    P.emit(nc, stack)
    stack.close()
    return nc


_NC_CACHE = {}


def _host_layouts(inp):
    f32 = np.float32
    ng = np.asarray(inp["norm_g"], f32)
    gains = np.ascontiguousarray(ng.reshape(6, KC, 128).transpose(2, 0, 1))
    wg = np.asarray(inp["ffn_w_gate"], f32).reshape(4, KC, 128, FC, 128)
    wu = np.asarray(inp["ffn_w_up"], f32).reshape(4, KC, 128, FC, 128)
    wgu = np.stack([wg, wu], axis=4)
    wgu = np.ascontiguousarray(wgu.transpose(0, 3, 2, 1, 4, 5)).reshape(4, FC, 128, KC * 2 * 128)
    wd = np.asarray(inp["ffn_w_down"], f32).reshape(4, NG, G, 128, D)
    wd = np.ascontiguousarray(wd.transpose(0, 1, 3, 2, 4))
    wi = np.asarray(inp["w_in_even"], f32)[0]
    secs = []
    for c0 in (0, 512, 1024, 1536, 2048, 2560, 4096, 4608):
        w = wi[:, c0:c0 + 512].reshape(KC, 128, 512).transpose(1, 0, 2)
        secs.append(w.reshape(128, 4, 2048))
    win_tm = np.ascontiguousarray(np.stack(secs, 0))
    g_rep = np.concatenate([np.tile(np.asarray(inp["q_norm_g"], f32)[0][None, :], (128, 1)),
                            np.tile(np.asarray(inp["k_norm_g"], f32)[0][None, :], (128, 1)),
                            np.tile(np.asarray(inp["a_v_norm_g"], f32)[0][None, :], (128, 1))], axis=1)
    au = wi[:, 3072:4096].reshape(KC, 128, 2, 4, 128).transpose(2, 1, 0, 3, 4)
    wau = np.ascontiguousarray(au).reshape(2, 128, 4, 2048)
    wo = np.asarray(inp["w_out_even"], f32)[0].reshape(KC, 128, 4, 512).transpose(2, 1, 0, 3)
    wout = np.ascontiguousarray(wo).reshape(4, 128, 4, 2048)
    wsT = np.asarray(inp["a_ws"], f32)[0].transpose(2, 0, 1).reshape(128, 1024)
    bs_rep = np.tile(np.asarray(inp["a_bs"], f32)[0].reshape(1, 1024), (128, 1))
    tri = (np.arange(128)[:, None] <= np.arange(128)[None, :]).astype(f32)
    gconst = np.ascontiguousarray(np.concatenate([bs_rep, tri], axis=1))
    wsT = np.ascontiguousarray(wsT)
    wo_ = np.asarray(inp["w_in_odd"], f32)[0].reshape(KC, 128, 4, 8, 128).transpose(3, 1, 0, 2, 4)
    win_odd = np.ascontiguousarray(wo_).reshape(8, 128, 4, 2048)
    woo = np.asarray(inp["w_out_odd"], f32)[0].reshape(KC, 128, 4, 512).transpose(2, 1, 0, 3)
    wout_odd = np.ascontiguousarray(woo).reshape(4, 128, 4, 2048)
    cw = np.asarray(inp["c_wg"], f32)[0].reshape(4, 2, 128, 2, 128).transpose(2, 0, 1, 3, 4)
    cwg = np.ascontiguousarray(cw).reshape(128, 2048)
    csc = np.asarray(inp["c_scale"], f32)[0].reshape(8, 128).T
    taps = np.asarray(inp["d_conv_w"], f32)[0].reshape(3, 8, 128).transpose(2, 1, 0).reshape(128, 24)
    odd_pack = (win_odd, wout_odd, cwg, np.ascontiguousarray(np.concatenate([csc, taps], axis=1)))
    ii = np.arange(128)
    aconst = np.concatenate([np.eye(128, dtype=f32), (ii[:, None] < ii[None, :]).astype(f32),
                             -(ii[:, None] >= ii[None, :]).astype(f32), -np.ones((128, 128), f32),
                             np.zeros((128, 128), f32),
                             np.tile(np.asarray(inp["sb_bias"], f32)[0][None, :], (128, 1)),
                             ii[:, None].astype(f32),
                             np.tile(np.repeat(np.asarray(inp["sb_bias"], f32)[0], 4)[None, :], (128, 1))], axis=1)
    return gains, wgu, wd, win_tm, np.ascontiguousarray(g_rep), wau, wout, gconst, np.ascontiguousarray(aconst), wsT, odd_pack


def kernel(**inp):
    f32 = np.float32
    xp = np.asarray(inp["x_prompt"], f32)
    xs = np.asarray(inp["x_sample"], f32)
    gains, wgu, wd, win_tm, g_rep, wau, wout, gconst, aconst, wsT, odd_pack = _host_layouts(inp)
    win_odd, wout_odd, cwg, oc32 = odd_pack
    ck = np.asarray(inp["cache_k"], f32)
    n_phys, n_pages = ck.shape[1], np.asarray(inp["page_table"]).shape[1]
    ck_rows = ck.reshape(n_phys * 128, 1024)
    cv_rows = np.asarray(inp["cache_v"], f32).reshape(n_phys * 128, 1024)
    in_maps = []
    for c in range(N_CORES):
        b, half = c // 2, c % 2
        start = half * TOK
        xt = np.zeros((NT, D), f32)
        if start >= NB:
            xt[0:NB] = xp[b, start - NB:start]
        xt[NB:NB + TOK] = xp[b, start:start + TOK]
        xt[NB + TOK:] = xs[c]
        xT = np.ascontiguousarray(xt.reshape(NT, KC, 128).transpose(2, 1, 0))
        xo = np.zeros((OTHER_ROWS, D), f32)
        if half == 1:
            xo[:] = xp[b, 0:OTHER_ROWS]
        xoT = np.ascontiguousarray(xo.reshape(OTHER_ROWS, KC, 128).transpose(2, 1, 0))
        wins = np.array([2, 4, 8, 16], f32)
        pos = (half * TOK + np.arange(16, dtype=f32))[None, :] + 1.0
        invc = np.tile((1.0 / np.minimum(pos, wins[:, None])).reshape(1, 64).astype(f32), (128, 1))
        oconst = np.ascontiguousarray(np.concatenate([oc32, invc], axis=1))
        pool_hist = np.ascontiguousarray(np.asarray(inp["state_pool"], f32)[0, c].reshape(15, 8, 128).transpose(2, 1, 0))
        conv_hist = np.ascontiguousarray(np.asarray(inp["state_conv"], f32)[0, c].reshape(2, 8, 128).transpose(2, 1, 0))
        pt_rep = np.ascontiguousarray(np.tile(np.asarray(inp["page_table"], np.int32)[c][None, :], (128, 1)))
        m = {"xT": xT, "gains": gains, "win_tm": win_tm, "g_rep": g_rep, "wau": wau, "wout": wout, "gconst": gconst, "aconst": aconst, "wsT": wsT,
             "xoT": xoT, "ck_rows": ck_rows, "cv_rows": cv_rows, "pt_rep": pt_rep, "win_odd": win_odd, "wout_odd": wout_odd,
             "cwg": cwg, "oconst": oconst, "pool_hist": pool_hist, "conv_hist": conv_hist}
        if not _DEV["skip_ffn"]:
            m.update({"wgu": wgu, "wd": wd})
        in_maps.append(m)
    if "nc" not in _NC_CACHE:
        _NC_CACHE["nc"] = build_program(n_phys, n_pages)
    res = run_bass_kernel_spmd(_NC_CACHE["nc"], in_maps, core_ids=list(range(N_CORES)))
    B, S = 4, 2048
    y_prompt = np.zeros((B, S, D), f32)
    y_sample = np.zeros((8, 4, D), f32)
    for c in range(N_CORES):
        b, half = c // 2, c % 2
        yT = np.asarray(res.results[c]["yT"])
        y = yT.transpose(2, 1, 0).reshape(NT, D)
        y_prompt[b, half * TOK:(half + 1) * TOK] = y[NB:NB + TOK]
        y_sample[c] = y[NB + TOK:]
    z = lambda *s: np.zeros(s, f32)
    nk_p, nv_p = z(1, 4, 2048, 8, 128), z(1, 4, 2048, 8, 128)
    nk_s, nv_s, nav_s = z(1, 8, 4, 8, 128), z(1, 8, 4, 8, 128), z(1, 8, 4, 1024)
    for c in range(N_CORES):
        b, half = c // 2, c % 2
        r = res.results[c]
        ko, vo = np.asarray(r["k_out"]), np.asarray(r["v_out"])
        nk_p[0, b, half * TOK:(half + 1) * TOK] = ko[NB:NB + TOK].reshape(TOK, 8, 128)
        nv_p[0, b, half * TOK:(half + 1) * TOK] = vo[NB:NB + TOK].reshape(TOK, 8, 128)
        nk_s[0, c] = ko[NB + TOK:].reshape(4, 8, 128)
        nv_s[0, c] = vo[NB + TOK:].reshape(4, 8, 128)
        nav_s[0, c] = np.asarray(r["avn_out"])
    pool_p, pool_s, conv_p, conv_s = z(1, 4, 15, 1024), z(1, 8, 15, 1024), z(1, 4, 2, 1024), z(1, 8, 2, 1024)
    for c in range(N_CORES):
        b, half = c // 2, c % 2
        ost = np.asarray(res.results[c]["ost"])
        tm = lambda a: a.transpose(2, 1, 0).reshape(a.shape[2], 1024)
        if half == 1:
            pool_p[0, b] = tm(ost[:, :, 0:15])
            conv_p[0, b] = tm(ost[:, :, 30:32])
        pool_s[0, c] = tm(ost[:, :, 15:30])
        conv_s[0, c] = tm(ost[:, :, 32:34])
    return (y_prompt, y_sample, nk_p, nv_p, nk_s, nv_s, nav_s, pool_p, pool_s, conv_p, conv_s)
```
